# Optimizing a Trainium2 kernel written in Bass

```python
import jax
import jax.numpy as jnp
from jax import lax
import numpy as np

D_MODEL = 4096
BATCH = 2
SEQ = 4096
DEPTH = 2

EPS = 1e-6
PLE_DIM = 256
CHUNK = 64
CHUNK_LOG2 = 6

GDN_HEADS = 16
GDN_DK = 128
GDN_DV = 128
GDN_QK = GDN_HEADS * GDN_DK
GDN_V = GDN_HEADS * GDN_DV
CONV_W = 4

GLA_HEADS = 16
GLA_DK = 64
GLA_DV = 128
GLA_QK = GLA_HEADS * GLA_DK
GLA_V = GLA_HEADS * GLA_DV
GLA_RANK = 16
GLA_TAU = 16.0

PEER_HEADS = 8
N_KEYS = 128
N_EXPERTS = N_KEYS * N_KEYS
D_KEY = 256
HALF_KEY = D_KEY // 2
TOPK_HALF = 16
TOPK = 16
PEER_BLOCK = 64
PEER_V_SCALE = 0.2

IN_SPLITS = (2 * GDN_QK + GDN_V, GDN_V, GDN_HEADS, GDN_HEADS, GLA_QK, GLA_QK, GLA_V, GLA_V, GLA_RANK, 2 * D_MODEL)
IN_WIDTH = sum(IN_SPLITS)

kernel_name = 'hybrid_gdn_gla_peer_ple'


def rms_norm(x, w):
    xf = x.astype(jnp.float32)
    y = xf * lax.rsqrt(jnp.mean(xf * xf, axis=-1, keepdims=True) + EPS)
    return (y * w.astype(jnp.float32)).astype(x.dtype)


def l2_norm(x):
    xf = x.astype(jnp.float32)
    return xf * lax.rsqrt(jnp.sum(xf * xf, axis=-1, keepdims=True) + EPS)


def causal_conv_silu(x, w):
    out = lax.conv_general_dilated(
        x, w[:, None, :].astype(x.dtype), window_strides=(1,), padding=[(CONV_W - 1, 0)],
        dimension_numbers=('NWC', 'WIO', 'NWC'), feature_group_count=x.shape[-1])
    return jax.nn.silu(out)


def _chunk(t):
    b, s, h, d = t.shape
    return t.reshape(b, s // CHUNK, CHUNK, h, d).transpose(1, 0, 3, 2, 4)


def _unchunk(t):
    n, b, h, c, d = t.shape
    return t.transpose(1, 0, 3, 2, 4).reshape(b, n * c, h, d)


def gated_delta_rule(q, k, v, log_decay, beta):
    q = q * GDN_DK ** -0.5
    qc, kc, vc = _chunk(q), _chunk(k), _chunk(v)
    gc = jnp.cumsum(_chunk(log_decay[..., None])[..., 0], axis=-1)
    bc = _chunk(beta[..., None])
    causal = jnp.tril(jnp.ones((CHUNK, CHUNK), dtype=bool))
    strict = jnp.tril(jnp.ones((CHUNK, CHUNK), dtype=bool), -1)
    decay = jnp.exp(jnp.where(causal, gc[..., :, None] - gc[..., None, :], -jnp.inf))
    k_beta = kc * bc
    n_mat = -jnp.where(strict, jnp.einsum('nbhid,nbhjd->nbhij', k_beta, kc) * decay, 0.0)
    t_inv = jnp.eye(CHUNK, dtype=q.dtype) + n_mat
    power = n_mat
    for _ in range(CHUNK_LOG2 - 1):
        power = power @ power
        t_inv = t_inv + t_inv @ power
    u = t_inv @ (vc * bc)
    w = t_inv @ (k_beta * jnp.exp(gc)[..., None])
    attn = jnp.einsum('nbhid,nbhjd->nbhij', qc, kc) * decay

    def step(state, inp):
        q_i, k_i, u_i, w_i, g_i, a_i = inp
        v_new = u_i - w_i @ state
        o_i = (q_i * jnp.exp(g_i)[..., None]) @ state + a_i @ v_new
        g_last = g_i[..., -1:]
        state = state * jnp.exp(g_last)[..., None] + jnp.einsum(
            'bhcd,bhce->bhde', k_i * jnp.exp(g_last - g_i)[..., None], v_new)
        return state, o_i

    s0 = jnp.zeros((q.shape[0], q.shape[2], GDN_DK, GDN_DV), q.dtype)
    _, o = lax.scan(step, s0, (qc, kc, u, w, gc, attn))
    return _unchunk(o)


def gla_chunked(q, k, v, log_a):
    q = q * GLA_DK ** -0.5
    qc, kc, vc = _chunk(q), _chunk(k), _chunk(v)
    bc = jnp.cumsum(_chunk(log_a), axis=-2)
    causal = jnp.tril(jnp.ones((CHUNK, CHUNK), dtype=bool))

    def step(state, inp):
        q_i, k_i, v_i, b_i = inp
        diff = b_i[..., :, None, :] - b_i[..., None, :, :]
        dec = jnp.exp(jnp.where(causal[:, :, None], diff, -jnp.inf))
        attn = jnp.einsum('bhid,bhjd,bhijd->bhij', q_i, k_i, dec)
        o_i = (q_i * jnp.exp(b_i)) @ state + attn @ v_i
        b_last = b_i[..., -1:, :]
        state = state * jnp.exp(b_last)[..., 0, :, None] + jnp.einsum(
            'bhcd,bhce->bhde', k_i * jnp.exp(b_last - b_i), v_i)
        return state, o_i

    s0 = jnp.zeros((q.shape[0], q.shape[2], GLA_DK, GLA_DV), q.dtype)
    _, o = lax.scan(step, s0, (qc, kc, vc, bc))
    return _unchunk(o)


def peer_ffn(h, w_query, keys1, keys2, expert_u, expert_v):
    b, s, d = h.shape
    n_tok = b * s
    tok = h.reshape(n_tok, d)
    q = (tok @ w_query).reshape(n_tok, PEER_HEADS, 2, HALF_KEY)
    s1 = jnp.einsum('thd,kd->thk', q[:, :, 0], keys1)
    s2 = jnp.einsum('thd,kd->thk', q[:, :, 1], keys2)
    v1, i1 = lax.top_k(s1, TOPK_HALF)
    v2, i2 = lax.top_k(s2, TOPK_HALF)
    cand_s = (v1[..., :, None] + v2[..., None, :]).reshape(n_tok, PEER_HEADS, TOPK_HALF * TOPK_HALF)
    cand_i = (i1[..., :, None] * N_KEYS + i2[..., None, :]).reshape(n_tok, PEER_HEADS, TOPK_HALF * TOPK_HALF)
    top_s, pos = lax.top_k(cand_s, TOPK)
    idx = jnp.take_along_axis(cand_i, pos, axis=-1)
    gate = jax.nn.softmax(top_s.astype(jnp.float32), axis=-1).astype(h.dtype)
    nb = n_tok // PEER_BLOCK

    def expert_block(args):
        x_blk, idx_blk, g_blk = args
        u = expert_u[idx_blk]
        act = jax.nn.gelu(jnp.einsum('td,thkd->thk', x_blk, u))
        v = expert_v[idx_blk]
        return jnp.einsum('thk,thkd->td', g_blk * act, v)

    out = lax.map(expert_block, (tok.reshape(nb, PEER_BLOCK, d),
                                 idx.reshape(nb, PEER_BLOCK, PEER_HEADS, TOPK),
                                 gate.reshape(nb, PEER_BLOCK, PEER_HEADS, TOPK)))
    return out.reshape(b, s, d)


def hybrid_layer(x, p_i, norm_mix, w_in, gdn_conv, gdn_a_log, gdn_dt_bias, gdn_norm,
                 gla_w_a2, gla_b_a, gla_norm, w_o_gdn, w_o_gla, w_out, norm_ffn,
                 peer_w_query, peer_keys1, peer_keys2, peer_u, peer_v,
                 norm_ple, w_ple, w_ple_gate):
    b, s, _ = x.shape
    f32 = jnp.float32
    h = rms_norm(x, norm_mix)
    split_points = np.cumsum(IN_SPLITS)[:-1].tolist()
    (g_qkv, g_z, g_beta, g_dt, l_q, l_k, l_v, l_r, l_a1, gate_logits) = jnp.split(h @ w_in, split_points, axis=-1)

    qkv = causal_conv_silu(g_qkv, gdn_conv)
    q, k, v = jnp.split(qkv, [GDN_QK, 2 * GDN_QK], axis=-1)
    q = l2_norm(q.reshape(b, s, GDN_HEADS, GDN_DK))
    k = l2_norm(k.reshape(b, s, GDN_HEADS, GDN_DK))
    v = v.reshape(b, s, GDN_HEADS, GDN_DV).astype(f32)
    beta = jax.nn.sigmoid(g_beta.astype(f32))
    log_decay = -jnp.exp(gdn_a_log.astype(f32)) * jax.nn.softplus(g_dt.astype(f32) + gdn_dt_bias.astype(f32))
    o_gdn = gated_delta_rule(q, k, v, log_decay, beta).astype(x.dtype)
    o_gdn = rms_norm(o_gdn, gdn_norm) * jax.nn.silu(g_z.reshape(b, s, GDN_HEADS, GDN_DV))
    y_gdn = o_gdn.reshape(b, s, GDN_V) @ w_o_gdn

    log_a = jax.nn.log_sigmoid((l_a1 @ gla_w_a2 + gla_b_a).astype(f32)) / GLA_TAU
    o_gla = gla_chunked(l_q.reshape(b, s, GLA_HEADS, GLA_DK).astype(f32),
                        l_k.reshape(b, s, GLA_HEADS, GLA_DK).astype(f32),
                        l_v.reshape(b, s, GLA_HEADS, GLA_DV).astype(f32),
                        log_a.reshape(b, s, GLA_HEADS, GLA_DK)).astype(x.dtype)
    o_gla = rms_norm(o_gla, gla_norm) * jax.nn.silu(l_r.reshape(b, s, GLA_HEADS, GLA_DV))
    y_gla = o_gla.reshape(b, s, GLA_V) @ w_o_gla

    gate_gdn, gate_gla = jnp.split(jax.nn.sigmoid(gate_logits), 2, axis=-1)
    x = x + (gate_gdn * y_gdn + gate_gla * y_gla) @ w_out

    x = x + peer_ffn(rms_norm(x, norm_ffn), peer_w_query, peer_keys1, peer_keys2, peer_u, peer_v)

    x = x + jax.nn.sigmoid(rms_norm(x, norm_ple) @ w_ple_gate) * (p_i @ w_ple)
    return x


def setup_inputs(seed: int = 0) -> dict:
    key = jax.random.key(seed)
    ks = jax.random.split(key, 32)

    def nrm(k, shape, scale):
        return jax.random.normal(k, shape, jnp.float32) * scale

    def gain(k, shape):
        return 1.0 + 0.01 * jax.random.normal(k, shape, jnp.float32)

    dt = jnp.exp(jax.random.uniform(ks[6], (DEPTH, GDN_HEADS), jnp.float32, np.log(1e-3), np.log(1e-1)))
    return {
        'x': nrm(ks[0], (BATCH, SEQ, D_MODEL), 1.0),
        'p': nrm(ks[1], (DEPTH, BATCH, SEQ, PLE_DIM), 1.0),
        'norm_mix': gain(ks[2], (DEPTH, D_MODEL)),
        'w_in': nrm(ks[3], (DEPTH, D_MODEL, IN_WIDTH), D_MODEL ** -0.5),
        'gdn_conv': nrm(ks[4], (DEPTH, CONV_W, 2 * GDN_QK + GDN_V), CONV_W ** -0.5),
        'gdn_a_log': jnp.log(jax.random.uniform(ks[5], (DEPTH, GDN_HEADS), jnp.float32, 1.0, 16.0)),
        'gdn_dt_bias': dt + jnp.log(-jnp.expm1(-dt)),
        'gdn_norm': gain(ks[7], (DEPTH, GDN_DV)),
        'gla_w_a2': nrm(ks[8], (DEPTH, GLA_RANK, GLA_QK), GLA_RANK ** -0.5),
        'gla_b_a': nrm(ks[9], (DEPTH, GLA_QK), 0.1),
        'gla_norm': gain(ks[10], (DEPTH, GLA_DV)),
        'w_o_gdn': nrm(ks[11], (DEPTH, GDN_V, D_MODEL), GDN_V ** -0.5),
        'w_o_gla': nrm(ks[12], (DEPTH, GLA_V, D_MODEL), GLA_V ** -0.5),
        'w_out': nrm(ks[13], (DEPTH, D_MODEL, D_MODEL), D_MODEL ** -0.5),
        'norm_ffn': gain(ks[14], (DEPTH, D_MODEL)),
        'peer_w_query': nrm(ks[15], (DEPTH, D_MODEL, PEER_HEADS * D_KEY), D_MODEL ** -0.5),
        'peer_keys1': nrm(ks[16], (DEPTH, N_KEYS, HALF_KEY), HALF_KEY ** -0.5),
        'peer_keys2': nrm(ks[17], (DEPTH, N_KEYS, HALF_KEY), HALF_KEY ** -0.5),
        'peer_u': nrm(ks[18], (DEPTH, N_EXPERTS, D_MODEL), D_MODEL ** -0.5),
        'peer_v': nrm(ks[19], (DEPTH, N_EXPERTS, D_MODEL), PEER_V_SCALE),
        'norm_ple': gain(ks[20], (DEPTH, D_MODEL)),
        'w_ple': nrm(ks[21], (DEPTH, PLE_DIM, D_MODEL), PLE_DIM ** -0.5),
        'w_ple_gate': nrm(ks[22], (DEPTH, D_MODEL, D_MODEL), D_MODEL ** -0.5),
        'norm_final': gain(ks[23], (D_MODEL,)),
    }


def reference(x, p, norm_mix, w_in, gdn_conv, gdn_a_log, gdn_dt_bias, gdn_norm,
              gla_w_a2, gla_b_a, gla_norm, w_o_gdn, w_o_gla, w_out, norm_ffn,
              peer_w_query, peer_keys1, peer_keys2, peer_u, peer_v,
              norm_ple, w_ple, w_ple_gate, norm_final):
    for i in range(DEPTH):
        x = hybrid_layer(x, p[i], norm_mix[i], w_in[i], gdn_conv[i], gdn_a_log[i], gdn_dt_bias[i],
                         gdn_norm[i], gla_w_a2[i], gla_b_a[i], gla_norm[i], w_o_gdn[i], w_o_gla[i],
                         w_out[i], norm_ffn[i], peer_w_query[i], peer_keys1[i], peer_keys2[i],
                         peer_u[i], peer_v[i], norm_ple[i], w_ple[i], w_ple_gate[i])
    return rms_norm(x, norm_final)
```

```python
from contextlib import ExitStack
from concourse.bass_utils import run_bass_kernel_spmd
import numpy as np
import concourse.bass as bass
import concourse.mybir as mybir

F32 = mybir.dt.float32
BF16 = mybir.dt.bfloat16
I32 = mybir.dt.int32
U32 = mybir.dt.uint32
AF = mybir.ActivationFunctionType
ALU = mybir.AluOpType
AX = mybir.AxisListType

SEM_ROT = 30000


class _Sem:
    def __init__(self, h, eng=None):
        self.h = h
        self.total = 0
        self.eng = eng


class Dep:
    __slots__ = ("name", "last_w", "readers", "dsem", "excl")

    def __init__(self, name, excl=False):
        self.name = name
        self.last_w = None
        self.readers = {}
        self.dsem = None
        self.excl = excl


class View:
    __slots__ = ("ap", "deps")

    def __init__(self, ap, deps):
        self.ap = ap
        self.deps = deps

    def __getitem__(self, idx):
        return View(self.ap[idx], self.deps)

    def rearrange(self, *a, **k):
        return View(self.ap.rearrange(*a, **k), self.deps)

    def broadcast(self, *a, **k):
        return View(self.ap.broadcast(*a, **k), self.deps)

    def to_broadcast(self, *a, **k):
        return View(self.ap.to_broadcast(*a, **k), self.deps)

    def bitcast(self, *a, **k):
        return View(self.ap.bitcast(*a, **k), self.deps)

    def key(self, dep):
        return View(self.ap, (dep,))


class T:
    def __init__(self, S, name, shape, dtype, space="sbuf"):
        self.S = S
        S.ntile = getattr(S, "ntile", 0) + 1
        name = f"{name}_u{S.ntile}"
        self.name = name
        self.space = space
        if space == "sbuf":
            self.t = S.tes.enter_context(S.nc.sbuf_tensor(name, list(shape), dtype))
        elif space == "psum":
            self.t = S.tes.enter_context(S.nc.psum_tensor(name, list(shape), dtype))
        else:
            self.t = S.nc.dram_tensor(name, list(shape), dtype, kind="Internal")
        self.dep = Dep(name, excl=(space == "psum"))
        self.subs = {}

    def __getitem__(self, idx):
        if self.space == "dram":
            return View(self.t.ap()[idx], (self.dep,))
        return View(self.t[idx], (self.dep,))

    def sub(self, key, idx):
        d = self.subs.get(key)
        if d is None:
            d = self.subs[key] = Dep(f"{self.name}.{key}")
        v = self[idx]
        return View(v.ap, (d,))


def _ap(x):
    return x.ap if isinstance(x, View) else x


def _deps(xs):
    out = []
    for x in xs:
        if isinstance(x, View):
            out.extend(x.deps)
    return out


class Sched:
    def __init__(self, nc, es):
        self.nc = nc
        self.es = es
        self.tes = es
        self.engs = {"pe": nc.tensor, "act": nc.scalar, "dve": nc.vector, "pool": nc.gpsimd, "sp": nc.sync}
        self.cur = {}
        self.waited = {e: {} for e in self.engs}
        self.nsem = 0
        self.all_dma_sems = []
        self.ninstr = 0
        for e in self.engs:
            self._rot(e)

    def _newsem(self, name, eng=None):
        self.nsem += 1
        return _Sem(self.es.enter_context(self.nc.semaphore(f"{name}_{self.nsem}")), eng)

    def _rot(self, e):
        self.cur[e] = self._newsem("e" + e, e)

    def tile(self, name, shape, dtype, space="sbuf"):
        return T(self, name, shape, dtype, space)

    def _wait(self, e, toks):
        best = {}
        for (s, v) in toks:
            if e == "pe" and s.eng == "pe":
                continue
            if best.get(s, 0) < v:
                best[s] = v
        w = self.waited[e]
        for s, v in best.items():
            if w.get(s, 0) < v:
                self.engs[e].wait_ge(s.h, v)
                w[s] = v

    def _collect(self, reads, writes):
        toks = []
        for d in reads:
            if d.last_w is not None:
                toks.append(d.last_w)
        for d in writes:
            if d.last_w is not None:
                toks.append(d.last_w)
            toks.extend(d.readers.items())
        return toks

    def _commit(self, tok, reads, writes):
        s, v = tok
        for d in reads:
            if d.readers.get(s, 0) < v:
                d.readers[s] = v
        for d in writes:
            d.last_w = tok
            d.readers = {}

    def op(self, e, fn, reads=(), writes=()):
        rd = _deps(reads)
        wd = _deps(writes)
        wd = wd + [d for d in rd if d.excl]
        self._wait(e, self._collect(rd, wd))
        ins = fn(self.engs[e])
        sem = self.cur[e]
        sem.total += 1
        ins.then_inc(sem.h, 1)
        tok = (sem, sem.total)
        self._commit(tok, rd, wd)
        self.ninstr += 1
        if sem.total >= SEM_ROT:
            self._rot(e)
        return tok

    def dma(self, q, out, in_, indirect=None, **kw):
        rd = _deps([in_] + ([indirect] if indirect is not None else []))
        wd = _deps([out])
        owner = None
        for d in (kw.pop("owner_deps", None) or (wd + rd)):
            owner = d
            break
        assert owner is not None
        if owner.dsem is None:
            owner.dsem = self._newsem("d")
            self.all_dma_sems.append(owner.dsem)
        sem = owner.dsem
        toks = self._collect(rd, wd)
        if sem.total:
            toks.append((sem, sem.total))
        self._wait(q, toks)
        eng = self.engs[q]
        if indirect is not None:
            ins = eng.indirect_dma_start(
                out=_ap(out), out_offset=None, in_=_ap(in_),
                in_offset=bass.IndirectOffsetOnAxis(ap=_ap(indirect), axis=0), **kw)
        else:
            ins = eng.dma_start(out=_ap(out), in_=_ap(in_), **kw)
        sem.total += 16
        ins.then_inc(sem.h, 16)
        tok = (sem, sem.total)
        self._commit(tok, rd, wd)
        self.ninstr += 1
        return tok

    def barrier(self):
        toks = [(s, s.total) for s in self.all_dma_sems if s.total]
        toks += [(s, s.total) for s in self.cur.values() if s.total]
        for e in self.engs:
            self._wait(e, [t for t in toks if t[0].eng != e])

    def finish(self, e="sp"):
        toks = [(s, s.total) for s in self.all_dma_sems if s.total]
        self._wait(e, toks)

    def mm(self, out, lhsT, rhs, start=True, stop=True, **kw):
        return self.op("pe", lambda e: e.matmul(_ap(out), _ap(lhsT), _ap(rhs), start=start, stop=stop, **kw),
                       reads=[lhsT, rhs] + ([] if start else [out]), writes=[out])

    def tr(self, out, in_, ident):
        return self.op("pe", lambda e: e.transpose(_ap(out), _ap(in_), _ap(ident)),
                       reads=[in_, ident], writes=[out])

    def act(self, out, in_, func, bias=None, scale=None, accum_out=None, eng="act"):
        kw = {}
        rd = [in_]
        wr = [out]
        if bias is not None:
            kw["bias"] = _ap(bias)
            rd.append(bias)
        if scale is not None:
            kw["scale"] = _ap(scale)
            rd.append(scale)
        if accum_out is not None:
            kw["accum_out"] = _ap(accum_out)
            wr.append(accum_out)
        return self.op(eng, lambda e: e.activation(_ap(out), _ap(in_), func, **kw), reads=rd, writes=wr)

    def tt(self, out, in0, in1, op, eng="dve"):
        return self.op(eng, lambda e: e.tensor_tensor(_ap(out), _ap(in0), _ap(in1), op),
                       reads=[in0, in1], writes=[out])

    def ts(self, out, in0, s1, op0, s2=None, op1=None, accum_out=None, eng="dve"):
        kw = {}
        wr = [out]
        if op1 is not None:
            kw["op1"] = op1
        if accum_out is not None:
            kw["accum_out"] = _ap(accum_out)
            wr.append(accum_out)
        return self.op(eng, lambda e: e.tensor_scalar(_ap(out), _ap(in0), _ap(s1), _ap(s2) if s2 is not None else None,
                                                      op0, **kw),
                       reads=[in0, s1, s2], writes=wr)

    def stt(self, out, in0, scalar, in1, op0, op1, accum_out=None, eng="dve"):
        kw = {}
        wr = [out]
        if accum_out is not None:
            kw["accum_out"] = _ap(accum_out)
            wr.append(accum_out)
        return self.op(eng, lambda e: e.scalar_tensor_tensor(_ap(out), _ap(in0), _ap(scalar), _ap(in1), op0, op1, **kw),
                       reads=[in0, scalar, in1], writes=wr)

    def copy(self, out, in_, eng="dve"):
        if eng == "act":
            return self.op("act", lambda e: e.copy(_ap(out), _ap(in_)), reads=[in_], writes=[out])
        return self.op(eng, lambda e: e.tensor_copy(_ap(out), _ap(in_)), reads=[in_], writes=[out])

    def memset(self, out, val, eng="dve"):
        return self.op(eng, lambda e: e.memset(_ap(out), val), reads=[], writes=[out])

    def recip(self, out, in_):
        return self.op("dve", lambda e: e.reciprocal(_ap(out), _ap(in_)), reads=[in_], writes=[out])

    def reduce(self, out, in_, op, axis=AX.X):
        return self.op("dve", lambda e: e.tensor_reduce(_ap(out), _ap(in_), axis, op), reads=[in_], writes=[out])


class Arena:
    def __init__(self, S, name, nslab, slab_f32=4096):
        self.T = S.tile(name, [128, nslab * slab_f32], F32)
        self.deps = [Dep(f"{name}{i}") for i in range(nslab)]
        self.sf = slab_f32

    def view(self, s0, ns=1, dtype=None):
        ap = self.T.t[:, s0 * self.sf:(s0 + ns) * self.sf]
        if dtype is not None and dtype != F32:
            ap = ap.bitcast(dtype)
        return View(ap, tuple(self.deps[s0:s0 + ns]))


CC_INC = 1


def _sched_collective(self, kind, src, dst, ngroup=8):
    rd = _deps([src])
    wd = _deps([dst])
    owner = wd[0]
    if owner.dsem is None:
        owner.dsem = self._newsem("c")
        self.all_dma_sems.append(owner.dsem)
    sem = owner.dsem
    toks = self._collect(rd, wd)
    if sem.total:
        toks.append((sem, sem.total))
    self._wait("pool", toks)
    ins = self.nc.gpsimd.collective_compute(kind, ALU.bypass, replica_groups=[list(range(ngroup))],
                                            ins=[_ap(src).opt()], outs=[_ap(dst).opt()])
    sem.total += CC_INC
    ins.then_inc(sem.h, CC_INC)
    tok = (sem, sem.total)
    self._commit(tok, rd, wd)
    self.ninstr += 1
    return tok


Sched.collective = _sched_collective


EPS = 1e-6
GELU_C = 1.5957691216057308


def tile_w(W):
    K, N = W.shape
    return np.ascontiguousarray(W.reshape(K // 128, 128, N // 128, 128).transpose(2, 1, 0, 3)).reshape(N // 128, 128, K)


def emit_B(S, PS, io, final, NTOK=1024, do_peer=True, n_slots=128):
    TB = 256
    TPB = TB // 128
    NB = NTOK // TB
    with ExitStack() as es:
        S.tes = es
        AR = Arena(S, "ar", 10)
        xres = AR.view(0, 2).rearrange("p (t d) -> p t d", t=TPB)
        hT = AR.view(2, 1, BF16).rearrange("p (k t) -> p k t", k=32)
        eq = AR.view(2, 1)[:, 0:2048].rearrange("p (h k a) -> p h k a", h=8, k=16)
        tmpx = AR.view(3, 1)
        cand = AR.view(3, 1)[:, 0:2048]
        candw = AR.view(3, 1)[:, 2048:4096]
        nbc = AR.view(4, 1)
        stage = [AR.view(5 + i, 1) for i in range(3)]
        ubuf = [AR.view(5 + i, 1, BF16)[:, 0:4096] for i in range(3)]
        djunk = AR.view(2, 1)
        s8b = AR.view(8, 1, BF16)
        ogT = s8b[:, 0:4096].rearrange("p (k t) -> p k t", k=16)
        olT = s8b[:, 4096:8192].rearrange("p (k t) -> p k t", k=16)
        sc = AR.view(8, 1)[:, 0:2048]
        scw = AR.view(8, 1)[:, 2048:4096]
        acc = AR.view(8, 1)
        mT = AR.view(9, 1, BF16).rearrange("p (k t) -> p k t", k=32)
        qT = AR.view(9, 1).rearrange("p (k t) -> p k t", k=16)
        wb = [S.tile(f"wb{i}", [128, 4096], BF16) for i in range(3)]
        ident = S.tile("ident_s", [128, 128], F32)
        iota16 = S.tile("iota16_s", [128, 16], F32)
        vecs = S.tile("vecs_s", [128, 96], F32)
        kT = S.tile("kT_s", [128, 2, 128], F32)
        pT = S.tile("pT", [128, 2, TB], BF16)
        pin = S.tile("pin", [128, 256], F32)
        t1 = [S.tile(f"t1_{i}", [128, TB], F32) for i in range(2)]
        t2 = [S.tile(f"t2_{i}", [128, TB], F32) for i in range(2)]
        dxT = [S.tile(f"dxT{i}", [128, TB], F32) for i in range(2)]
        ss = S.tile("ss", [128, 4], F32)
        rs = S.tile("rs", [128, 4], F32)
        v16 = S.tile("v16", [128, 16, 16], F32)
        i16 = S.tile("i16", [128, 16, 16], U32)
        i16f = S.tile("i16f", [128, 16, 16], F32)
        tv = S.tile("tv", [128, 8, 16], F32)
        pos = S.tile("pos", [128, 8, 16], U32)
        pa = S.tile("pa", [128, 8, 16], U32)
        pb = S.tile("pb", [128, 8, 16], U32)
        paf = S.tile("paf", [128, 8, 16], F32)
        pbf = S.tile("pbf", [128, 8, 16], F32)
        sel1 = S.tile("sel1", [128, 8, 16], F32)
        sel2 = S.tile("sel2", [128, 8, 16], F32)
        idxf = S.tile("idxf", [128, 128], F32)
        idxi = S.tile("idxi", [128, 128], I32)
        gate = S.tile("gate", [128, 8, 16], F32)
        gsum = S.tile("gsum", [128, 8], F32)
        dots = S.tile("dots", [128, 128], F32)
        ga = S.tile("ga", [128, 128], F32)
        gb = S.tile("gb", [128, 128], F32)
        identb = S.tile("identb", [128, 128], BF16)
        dg = [S.tile(f"dg{k}", [128, 128], BF16) for k in range(4)]

        S.dma("sp", ident[:], io["ident"][:, :])
        S.copy(identb[:], ident[:])
        S.dma("sp", iota16[:], io["iota16"][:, :])
        S.dma("sp", vecs[:], io["vecs"][:, :])
        S.dma("sp", kT[:], io["kT"].rearrange("c p k -> p c k"))

        wcnt = [0]

        def getw(w_blk, K):
            i = wcnt[0] % 3
            wcnt[0] += 1
            S.dma("sp", stage[i][:, 0:K], w_blk)
            S.copy(wb[i][:, 0:K], stage[i][:, 0:K], eng=("pool" if wcnt[0] % 2 else "act"))
            return wb[i]

        trc = [0]

        def to_feat(src, dst, tok_off, nk, wcol=None):
            for g in range(0, nk, 4):
                n = min(4, nk - g)
                bank = PS[trc[0] % 2]
                trc[0] += 1
                for j in range(n):
                    S.tr(bank[:, j * 128:(j + 1) * 128], (src(g + j) if callable(src) else src[:, (g + j) * 128:(g + j + 1) * 128]), ident[:])
                o = dst[:, g:g + n, tok_off:tok_off + 128]
                i_ = bank[:, 0:n * 128].rearrange("p (k t) -> p k t", k=n)
                if wcol is not None:
                    w_ = wcol[:, g:g + n].rearrange("p (k o) -> p k o", o=1).to_broadcast([128, n, 128])
                    S.tt(o, i_, w_, ALU.mult)
                else:
                    S.copy(o, i_, eng=("act" if (trc[0] % 2) else "dve"))

        def rstd_of(xv, col):
            S.act(tmpx, xv, AF.Square, accum_out=ss[:, col:col + 1])
            S.ts(ss[:, col:col + 1], ss[:, col:col + 1], 1.0 / 4096, ALU.mult, EPS, ALU.add)
            S.act(ss[:, col:col + 1], ss[:, col:col + 1], AF.Sqrt)
            S.recip(rs[:, col:col + 1], ss[:, col:col + 1])

        def norm_to_hT(wcol):
            for t in range(TPB):
                rstd_of(xres[:, t, :], t)
                S.ts(tmpx, xres[:, t, :], rs[:, t:t + 1], ALU.mult)
                to_feat(tmpx, hT, t * 128, 32, wcol)

        def back_add(f, src):
            bank = PS[4 + f % 2]
            for t in range(TPB):
                S.tr(bank[:, t * 128:(t + 1) * 128], src[:, t * 128:(t + 1) * 128], ident[:])
            xv = xres[:, :, f * 128:(f + 1) * 128]
            S.tt(xv, xv, bank[:, 0:TPB * 128].rearrange("p (t c) -> p t c", t=TPB), ALU.add)

        def vmax(o, i_):
            S.op("dve", lambda e: e.max(_ap(o), _ap(i_)), reads=[i_], writes=[o])

        def vmaxidx(o, m, i_):
            S.op("dve", lambda e: e.max_index(_ap(o), _ap(m), _ap(i_)), reads=[m, i_], writes=[o])

        def vrepl(o, m, i_):
            S.op("dve", lambda e: e.match_replace(_ap(o), _ap(m), _ap(i_), -1e30), reads=[m, i_], writes=[o])

        def top16(vals, idxs, src, work):
            vmax(vals[:, 0:8], src)
            vmaxidx(idxs[:, 0:8], vals[:, 0:8], src)
            vrepl(work, vals[:, 0:8], src)
            vmax(vals[:, 8:16], work)
            vmaxidx(idxs[:, 8:16], vals[:, 8:16], work)

        if do_peer:
            pub = S.tile("pub_i", [16384, 4096], BF16, "dram")
            pvb = S.tile("pvb_i", [16384, 4096], BF16, "dram")
            cc = 0
            for (src_t, dst_t) in ((io["pu"], pub), (io["pv"], pvb)):
                for r in range(128):
                    i = cc % 3
                    S.dma("sp", stage[i], src_t[r * 128:(r + 1) * 128, :])
                    S.copy(wb[i][:], stage[i], eng=("act", "dve", "pool")[cc % 3])
                    S.dma("pool", dst_t.sub(r, (slice(r * 128, (r + 1) * 128), slice(None))), wb[i][:],
                          owner_deps=_deps([wb[i][:]]))
                    cc += 1
            S.barrier()
        for blk in range(NB):
            r0 = blk * TB
            for t in range(TPB):
                io["load_x"](xres[:, t, :], blk * TPB + t)
            norm_to_hT(vecs[:, 0:32])
            for t in range(TPB):
                ogc, olc = io["load_o"](tmpx, blk * TPB + t)
                to_feat(ogc, ogT, t * 128, 16)
                to_feat(olc, olT, t * 128, 16)
            for f in range(32):
                b0 = (f % 2) * 4
                w = getw(io["wog"][f], 2048)
                for kc in range(16):
                    S.mm(PS[b0][:, 0:TB], w[:, kc * 128:(kc + 1) * 128], ogT[:, kc, :], start=(kc == 0), stop=(kc == 15))
                w = getw(io["wol"][f], 2048)
                for kc in range(16):
                    S.mm(PS[b0 + 1][:, 0:TB], w[:, kc * 128:(kc + 1) * 128], olT[:, kc, :], start=(kc == 0), stop=(kc == 15))
                w = getw(io["wg"][f], 4096)
                for kc in range(32):
                    S.mm(PS[b0 + 2][:, 0:TB], w[:, kc * 128:(kc + 1) * 128], hT[:, kc, :], start=(kc == 0), stop=(kc == 31))
                w = getw(io["wg"][32 + f], 4096)
                for kc in range(32):
                    S.mm(PS[b0 + 3][:, 0:TB], w[:, kc * 128:(kc + 1) * 128], hT[:, kc, :], start=(kc == 0), stop=(kc == 31))
                a1 = t1[f % 2]; a2 = t2[f % 2]
                S.act(a1[:], PS[b0 + 2][:, 0:TB], AF.Sigmoid)
                S.act(a2[:], PS[b0 + 3][:, 0:TB], AF.Sigmoid)
                S.tt(a1[:], a1[:], PS[b0][:, 0:TB], ALU.mult)
                S.tt(a2[:], a2[:], PS[b0 + 1][:, 0:TB], ALU.mult)
                S.tt(mT[:, f, :], a1[:], a2[:], ALU.add, eng="pool")
            for f in range(32):
                w = getw(io["wout"][f], 4096)
                for kc in range(32):
                    S.mm(PS[f % 2][:, 0:TB], w[:, kc * 128:(kc + 1) * 128], mT[:, kc, :], start=(kc == 0), stop=(kc == 31))
                S.copy(dxT[f % 2][:], PS[f % 2][:, 0:TB], eng="act")
                back_add(f, dxT[f % 2])
            if do_peer:
                S.dma("sp", nbc, io["nffn"].partition_broadcast(128))
                norm_to_hT(vecs[:, 32:64])
                for f in range(16):
                    w = getw(io["wq"][f], 4096)
                    for kc in range(32):
                        S.mm(PS[f % 2][:, 0:TB], w[:, kc * 128:(kc + 1) * 128], hT[:, kc, :], start=(kc == 0), stop=(kc == 31))
                    S.copy(qT[:, f, :], PS[f % 2][:, 0:TB], eng="act")
                for t in range(TPB):
                    for c in range(16):
                        S.mm(PS[4 + c // 4][:, (c % 4) * 128:(c % 4 + 1) * 128], qT[:, c, t * 128:(t + 1) * 128], kT[:, c % 2, :])
                    for b in range(4):
                        S.copy(sc[:, b * 512:(b + 1) * 512], PS[4 + b][:, 0:512], eng=("act" if b % 2 else "dve"))
                    for c in range(16):
                        top16(v16[:, c, :], i16[:, c, :], sc[:, c * 128:(c + 1) * 128], scw[:, c * 128:(c + 1) * 128])
                    S.copy(i16f[:], i16[:])
                    v4 = v16[:].rearrange("p (h c) k -> p h c k", c=2)
                    i4 = i16f[:].rearrange("p (h c) k -> p h c k", c=2)
                    cand4 = cand.rearrange("p (h a b) -> p h a b", h=8, a=16)
                    S.tt(cand4,
                         v4[:, :, 0, :].rearrange("p h (a o) -> p h a o", o=1).to_broadcast([128, 8, 16, 16]),
                         v4[:, :, 1, :].rearrange("p h (o b) -> p h o b", o=1).to_broadcast([128, 8, 16, 16]),
                         ALU.add)
                    for h in range(8):
                        top16(tv[:, h, :], pos[:, h, :], cand[:, h * 256:(h + 1) * 256], candw[:, h * 256:(h + 1) * 256])
                    S.ts(pa[:], pos[:], 4, ALU.logical_shift_right)
                    S.ts(pb[:], pos[:], 15, ALU.bitwise_and)
                    S.copy(paf[:], pa[:])
                    S.copy(pbf[:], pb[:])
                    io4 = iota16[:].rearrange("p (h k a) -> p h k a", h=1, k=1).to_broadcast([128, 8, 16, 16])
                    for (pf, half, sel) in ((paf, 0, sel1), (pbf, 1, sel2)):
                        S.tt(eq, pf[:].rearrange("p h (k o) -> p h k o", o=1).to_broadcast([128, 8, 16, 16]), io4, ALU.is_equal)
                        S.tt(eq, eq, i4[:, :, half, :].rearrange("p h (o a) -> p h o a", o=1).to_broadcast([128, 8, 16, 16]), ALU.mult)
                        S.reduce(sel[:], eq, ALU.add)
                    S.stt(idxf[:], sel1[:].rearrange("p h k -> p (h k)"), 128.0, sel2[:].rearrange("p h k -> p (h k)"), ALU.mult, ALU.add)
                    S.copy(idxi[:], idxf[:])
                    S.tt(gate[:], tv[:], tv[:, :, 0:1].to_broadcast([128, 8, 16]), ALU.subtract)
                    S.act(gate[:], gate[:], AF.Exp)
                    S.reduce(gsum[:], gate[:], ALU.add)
                    S.recip(gsum[:], gsum[:])
                    S.tt(gate[:], gate[:], gsum[:].rearrange("p (h o) -> p h o", o=1).to_broadcast([128, 8, 16]), ALU.mult)
                    hn = tmpx
                    S.stt(hn, xres[:, t, :], rs[:, t:t + 1], nbc, ALU.mult, ALU.mult)
                    for s in range(n_slots):
                        ub = ubuf[s % 3]
                        S.dma("pool", ub, pub[:, :], indirect=idxi[:, s:s + 1])
                        S.stt(djunk, ub, 1.0, hn, ALU.mult, ALU.mult, accum_out=dots[:, s:s + 1])
                    S.tt(ga[:], dots[:], dots[:], ALU.mult)
                    S.ts(ga[:], ga[:], 0.044715, ALU.mult, 1.0, ALU.add)
                    S.tt(ga[:], ga[:], dots[:], ALU.mult)
                    S.act(ga[:], ga[:], AF.Sigmoid, scale=GELU_C)
                    S.tt(ga[:], ga[:], dots[:], ALU.mult)
                    S.tt(gb[:], ga[:], gate[:].rearrange("p h k -> p (h k)"), ALU.mult)
                    for s in range(n_slots):
                        vb = ubuf[s % 3]
                        S.dma("pool", vb, pvb[:, :], indirect=idxi[:, s:s + 1])
                        dgt = dg[s % 4]
                        S.ts(dgt[:], identb[:], gb[:, s:s + 1], ALU.mult, eng=("pool" if s % 2 else "dve"))
                        for n in range(8):
                            S.mm(PS[n][:, 0:512], dgt[:], vb[:, n * 512:(n + 1) * 512], start=(s == 0), stop=(s == n_slots - 1))
                    for n in range(8):
                        xv = xres[:, t, n * 512:(n + 1) * 512]
                        S.tt(xv, xv, PS[n][:, 0:512], ALU.add)
            norm_to_hT(vecs[:, 64:96])
            for t in range(TPB):
                io["load_p"](pin[:], blk * TPB + t)
                to_feat(pin[:], pT, t * 128, 2)
            for f in range(32):
                w = getw(io["wpg"][f], 4096)
                for kc in range(32):
                    S.mm(PS[f % 2][:, 0:TB], w[:, kc * 128:(kc + 1) * 128], hT[:, kc, :], start=(kc == 0), stop=(kc == 31))
                w = getw(io["wple"][f], 256)
                for kc in range(2):
                    S.mm(PS[2 + f % 2][:, 0:TB], w[:, kc * 128:(kc + 1) * 128], pT[:, kc, :], start=(kc == 0), stop=(kc == 1))
                a1 = t1[f % 2]
                S.act(a1[:], PS[f % 2][:, 0:TB], AF.Sigmoid)
                S.tt(dxT[f % 2][:], a1[:], PS[2 + f % 2][:, 0:TB], ALU.mult)
                back_add(f, dxT[f % 2])
            if final:
                S.dma("sp", nbc, io["nfin"].partition_broadcast(128))
            for t in range(TPB):
                if final:
                    rstd_of(xres[:, t, :], t)
                    S.stt(tmpx, xres[:, t, :], rs[:, t:t + 1], nbc, ALU.mult, ALU.mult)
                    io["store_x"](blk * TPB + t, tmpx)
                else:
                    io["store_x"](blk * TPB + t, xres[:, t, :])
        S.barrier()


def build_B(final, NTOK=1024, do_peer=True, n_slots=128):
    nc = bass.Bass("TRN2", target_bir_lowering=False)

    def din(name, shape, dt=F32):
        return nc.dram_tensor(name, list(shape), dt, kind="ExternalInput").ap()

    x_d = din("x", [NTOK, 4096]); og_d = din("og", [NTOK, 2048]); ol_d = din("ol", [NTOK, 2048]); p_d = din("p", [NTOK, 256])
    io = dict(wg=din("wg", [64, 128, 4096]), wog=din("wog", [32, 128, 2048]), wol=din("wol", [32, 128, 2048]),
              wout=din("wout", [32, 128, 4096]), wq=din("wq", [16, 128, 4096]), wpg=din("wpg", [32, 128, 4096]),
              wple=din("wple", [32, 128, 256]), kT=din("kT", [2, 128, 128]),
              pu=din("pu", [16384, 4096]), pv=din("pv", [16384, 4096]),
              vecs=din("vecs", [128, 96]), nffn=din("nffn", [4096]), nfin=din("nfin", [4096]),
              ident=din("ident", [128, 128]), iota16=din("iota16", [128, 16]))
    out_d = nc.dram_tensor("xo", [NTOK, 4096], F32, kind="ExternalOutput").ap()
    with ExitStack() as es0:
        S = Sched(nc, es0)
        PS = [S.tile(f"ps{i}", [128, 512], F32, "psum") for i in range(8)]

        def load_o(tmpx, t):
            S.dma("sp", tmpx[:, 0:2048], og_d[t * 128:(t + 1) * 128, :])
            S.dma("sp", tmpx[:, 2048:4096], ol_d[t * 128:(t + 1) * 128, :])
            return (lambda k: tmpx[:, k * 128:(k + 1) * 128]), (lambda k: tmpx[:, 2048 + k * 128:2048 + (k + 1) * 128])

        io["load_x"] = lambda v, t: S.dma("sp", v, x_d[t * 128:(t + 1) * 128, :])
        io["load_o"] = load_o
        io["load_p"] = lambda v, t: S.dma("sp", v, p_d[t * 128:(t + 1) * 128, :])
        io["store_x"] = lambda t, v: S.dma("pool", out_d[t * 128:(t + 1) * 128, :], v)
        emit_B(S, PS, io, final, NTOK=NTOK, do_peer=do_peer, n_slots=n_slots)
        S.finish("pool")
        print("B instrs", S.ninstr, "sems", S.nsem)
    return nc


C = 64
NCHUNKS_A = 33
CQ, CK, CV, CZ, LQ, LK, LV, LR, CM = 0, 4, 8, 12, 16, 20, 24, 28, 32


def a_weight_cols(g):
    cols = []
    hs = [4 * g + i for i in range(4)]
    for base in (0, 2048, 4096, 6144):
        for h in hs:
            cols.append(np.arange(base + h * 128, base + (h + 1) * 128))
    for base in (8224, 9248):
        for h in hs:
            c = np.full(128, -1)
            c[:64] = np.arange(base + h * 64, base + (h + 1) * 64)
            cols.append(c)
    for base in (10272, 12320):
        for h in hs:
            cols.append(np.arange(base + h * 128, base + (h + 1) * 128))
    c = np.full(128, -1)
    c[0:16] = np.arange(14368, 14384)
    c[32:36] = 8192 + np.array(hs)
    c[64:68] = 8208 + np.array(hs)
    cols.append(c)
    return cols


def a_tile_weights(w_in_l, g):
    cols = np.concatenate(a_weight_cols(g))
    Wsel = np.zeros((4096, cols.size), np.float32)
    m = cols >= 0
    Wsel[:, m] = w_in_l[:, cols[m]]
    return tile_w(Wsel)


def a_consts():
    i = np.arange(C)
    incl = (i[:, None] >= i[None, :])
    cst = {}
    cst["negm"] = np.where(incl, 0.0, -1e30).astype(np.float32)
    cst["negmT"] = np.where(incl.T, 0.0, -1e30).astype(np.float32)
    cst["strict"] = (i[:, None] > i[None, :]).astype(np.float32)
    cst["inclT"] = incl.T.astype(np.float32)
    return np.concatenate([cst["negm"], cst["negmT"], cst["strict"], cst["inclT"]], axis=1)


def emit_A(S, PS, io, NSEQ=4096, stop=99):
    TB = 256
    TPB = TB // 128
    NB = NSEQ // TB
    CPB = TB // C
    with ExitStack() as es:
        S.tes = es
        AR = Arena(S, "ar", 3)
        xin = [AR.view(0, 1), AR.view(0, 1)]
        stage = [AR.view(1 + i, 1) for i in range(2)]
        hTt = S.tile("hT", [128, 32, TB], BF16)
        hT = hTt[:]
        wb = [S.tile(f"wb{i}", [128, 4096], BF16) for i in range(2)]
        ident = S.tile("ident_s", [128, 128], F32)
        ones = S.tile("ones_s", [128, 128], F32)
        vecs = S.tile("vecs_s", [128, 32], F32)
        cw = S.tile("cw_s", [128, 12, 4], F32)
        cst = S.tile("cst_s", [64, 256], F32)
        hv = S.tile("hv_s", [128, 12], F32)
        gn = S.tile("gn_s", [128, 256], F32)
        wa2 = S.tile("wa2_s", [16, 4, 64], F32)
        ba = S.tile("ba_s", [64, 4], F32)
        nba = S.tile("nba_s", [64, 4], F32)
        rmask = S.tile("rmask_s", [64, TB], F32)
        negA = S.tile("negA", [128, 4], F32)
        pre = [S.tile(f"pre{f}", [128, 3 + TB], F32) for f in range(12)]
        fm = [S.tile(f"fm{f}", [128, TB], F32) for f in range(NCHUNKS_A)]
        ssq = [S.tile(f"ssq{i}", [128, TB], F32) for i in range(2)]
        ss = S.tile("ss", [128, 4], F32)
        rs = S.tile("rs", [128, 4], F32)
        bT = [S.tile(f"bT{i}", [64, TB], F32) for i in range(4)]
        qb = [S.tile(f"qb{i}", [64, TB], F32) for i in range(4)]
        kb = [S.tile(f"kb{i}", [64, TB], F32) for i in range(4)]
        kdT = [S.tile(f"kdT{i}", [64, TB], F32) for i in range(4)]
        ebl = [S.tile(f"ebl{i}", [64, CPB], F32) for i in range(4)]
        gtmp = [S.tile(f"gtmp{i}", [64, TB], F32) for i in range(2)]
        Sg = [S.tile(f"Sg{i}", [128, 128], F32) for i in range(4)]
        Sl = [S.tile(f"Sl{i}", [64, 128], F32) for i in range(4)]
        def two(name, shape):
            return [S.tile(f"{name}{k}", shape, F32) for k in range(2)]

        shared2 = [two("mt", [64, 128]), two("beta4", [64, 4]), two("nbeta4", [64, 4]), two("ld4", [64, 4]), two("g4", [64, 4]),
                   two("ng4", [64, 4]), two("eg4", [64, 4]), two("bg4", [64, 4]), two("dk4", [64, 4])]
        diag = S.tile("diag", [64, 4, 64], F32)
        gbc = S.tile("gbc", [128, 4, 64], F32)
        egbc = S.tile("egbc", [128, 4, 64], F32)

        def hb(name, shape):
            return [S.tile(f"{name}{i}", shape, F32) for i in range(4)]

        tA = hb("tA", [64, 64]); dec = hb("dec", [64, 64]); decT = hb("decT", [64, 64])
        Nm = hb("Nm", [64, 64]); Mm = hb("Mm", [64, 64]); Q2 = hb("Q2", [64, 64]); QT2 = hb("QT2", [64, 64])
        X = hb("X", [64, 64]); attnT = hb("attnT", [64, 64])
        vbt = hb("vbt", [64, 128]); kbg = hb("kbg", [64, 128]); u = hb("u", [64, 128]); wT = hb("wT", [128, 64])
        qgT = hb("qgT", [128, 64]); kdec = hb("kdec", [64, 128]); vnew = hb("vnew", [64, 128]); o1 = hb("o1", [64, 128])
        lattn = hb("lattn", [64, 64]); lvtok = hb("lvtok", [64, 128]); lkd = hb("lkd", [64, 64])
        osb = [S.tile(f"osb{i}", [64, 512], F32) for i in range(2)]
        lsb = [S.tile(f"lsb{i}", [64, 512], F32) for i in range(2)]
        oss = S.tile("oss", [64, 8], F32)
        ors = S.tile("ors", [64, 8], F32)
        sz = [S.tile(f"sz{k}", [64, 128], F32) for k in range(8)]
        ojunk = sz
        lo1 = hb("lo1", [64, 128]); tB = hb("tB", [64, 64])

        for (t_, d_) in ((ident, io["ident"]), (vecs, io["vecs"]), (cst, io["cst"]), (hv, io["hv"]), (gn, io["gn"]), (ba, io["ba"]), (rmask, io["rmask"])):
            S.dma("sp", t_[:], d_[:, :])
        S.dma("sp", cw[:], io["cw"][:, :, :])
        S.dma("sp", wa2[:], io["wa2"][:, :, :])
        S.memset(ones[:], 1.0)
        S.ts(nba[:], ba[:], -1.0, ALU.mult)
        S.act(negA[:], hv[:, 0:4], AF.Exp)
        S.ts(negA[:], negA[:], -1.0, ALU.mult)
        for f in range(12):
            S.memset(pre[f][:, 0:3], 0.0)
        for i in range(4):
            S.memset(Sg[i][:], 0.0)
            S.memset(Sl[i][:], 0.0)
        negm = cst[:, 0:64]; negmT = cst[:, 64:128]; strict = cst[:, 128:192]; inclT = cst[:, 192:256]
        id64 = ident[0:64, 0:64]

        wcnt = [0]

        def getw(w_blk, K):
            i = wcnt[0] % 2
            wcnt[0] += 1
            S.dma("sp", stage[i][:, 0:K], w_blk)
            S.copy(wb[i][:, 0:K], stage[i][:, 0:K], eng=("pool" if wcnt[0] % 2 else "act"))
            return wb[i]

        trc = [0]

        def to_feat(src, dst, tok_off, nk, wcol):
            for g in range(0, nk, 4):
                n = min(4, nk - g)
                bank = PS[trc[0] % 2]
                trc[0] += 1
                for j in range(n):
                    S.tr(bank[:, j * 128:(j + 1) * 128], src[:, (g + j) * 128:(g + j + 1) * 128], ident[:])
                o = dst[:, g:g + n, tok_off:tok_off + 128]
                i_ = bank[:, 0:n * 128].rearrange("p (k t) -> p k t", k=n)
                w_ = wcol[:, g:g + n].rearrange("p (k o) -> p k o", o=1).to_broadcast([128, n, 128])
                S.tt(o, i_, w_, ALU.mult)

        evc = [0]

        def evac(dst, src):
            evc[0] += 1
            S.copy(dst, src, eng=("act" if evc[0] % 2 else "dve"))

        for blk in range(NB):
            r0 = blk * TB
            for t in range(TPB):
                xv = xin[t % 2]
                io["load_x"](xv, blk * TPB + t)
                S.act(wb[0][:], xv, AF.Square, accum_out=ss[:, t:t + 1])
                S.ts(ss[:, t:t + 1], ss[:, t:t + 1], 1.0 / 4096, ALU.mult, EPS, ALU.add)
                S.act(ss[:, t:t + 1], ss[:, t:t + 1], AF.Sqrt)
                S.recip(rs[:, t:t + 1], ss[:, t:t + 1])
                S.ts(xv, xv, rs[:, t:t + 1], ALU.mult)
                to_feat(xv, hT, t * 128, 32, vecs[:, 0:32])
            for f in range(NCHUNKS_A):
                w = getw(io["wa"][f], 4096)
                bank = PS[2 + f % 2]
                for kc in range(32):
                    S.mm(bank[:, 0:TB], w[:, kc * 128:(kc + 1) * 128], hT[:, kc, :], start=(kc == 0), stop=(kc == 31))
                if f < 12:
                    evac(pre[f][:, 3:3 + TB], bank[:, 0:TB])
                else:
                    evac(fm[f][:], bank[:, 0:TB])
            if stop < 1:
                continue
            for f in range(12):
                o = fm[f]
                S.ts(o[:], pre[f][:, 0:TB], cw[:, f, 0:1], ALU.mult)
                for k in range(1, 4):
                    S.stt(o[:], pre[f][:, k:k + TB], cw[:, f, k:k + 1], o[:], ALU.mult, ALU.add)
                S.copy(pre[f][:, 0:3], pre[f][:, TB:TB + 3], eng="pool")
                S.act(o[:], o[:], AF.Silu)
                if f < 8:
                    sq = ssq[f % 2]
                    S.act(sq[:], o[:], AF.Square)
                    bank = PS[4 + f % 2]
                    S.mm(bank[:, 0:TB], ones[:], sq[:])
                    S.ts(sq[:], bank[:, 0:TB], EPS, ALU.add)
                    S.act(sq[:], sq[:], AF.Sqrt)
                    S.recip(sq[:], sq[:])
                    if f < 4:
                        S.stt(o[:], o[:], 128 ** -0.5, sq[:], ALU.mult, ALU.mult)
                    else:
                        S.tt(o[:], o[:], sq[:], ALU.mult)
            if stop < 2:
                continue
            for i in range(4):
                bank = PS[4 + i % 2]
                S.mm(bank[0:64, 0:TB], wa2[:, i, :], fm[CM][0:16, :])
                t_ = gtmp[i % 2]
                S.act(t_[:], bank[0:64, 0:TB], AF.Exp, bias=nba[:, i:i + 1], scale=-1.0)
                S.act(t_[:], t_[:], AF.Ln, bias=1.0)
                S.ts(t_[:], t_[:], -1.0 / 16.0, ALU.mult)
                S.op("dve", lambda e, i=i, t_=t_: e.tensor_tensor_scan(_ap(bT[i][:]), _ap(rmask[:]), _ap(t_[:]), 0.0, ALU.mult, ALU.add),
                     reads=[rmask[:], t_[:]], writes=[bT[i][:]])
                eb = gtmp[(i + 1) % 2]
                S.act(eb[:], bT[i][:], AF.Exp)
                S.stt(qb[i][:], fm[LQ + i][0:64, :], 64 ** -0.5, eb[:], ALU.mult, ALU.mult)
                S.act(eb[:], bT[i][:], AF.Exp, scale=-1.0)
                S.tt(kb[i][:], fm[LK + i][0:64, :], eb[:], ALU.mult)
                b3 = bT[i][:].rearrange("p (c t) -> p c t", t=C)
                S.tt(eb[:].rearrange("p (c t) -> p c t", t=C), b3[:, :, C - 1:C].to_broadcast([64, CPB, C]), b3, ALU.subtract)
                S.act(eb[:], eb[:], AF.Exp)
                S.tt(kdT[i][:], fm[LK + i][0:64, :], eb[:], ALU.mult)
                S.act(ebl[i][:], b3[:, :, C - 1], AF.Exp)
            if stop < 3:
                continue
            for c in range(CPB):
                cs = slice(c * C, (c + 1) * C)
                par = c % 2
                tok0 = r0 + c * C
                mt, beta4, nbeta4, ld4, g4, ng4, eg4, bg4, dk4 = [x[par] for x in shared2]
                bank = PS[4]
                S.tr(bank[0:64, 0:128], fm[CM][:, cs], ident[:])
                S.copy(mt[:], bank[0:64, 0:128], eng="act")
                S.act(beta4[:], mt[:, 32:36], AF.Sigmoid)
                S.ts(nbeta4[:], beta4[:], -1.0, ALU.mult)
                S.tt(ld4[:], mt[:, 64:68], hv[0:64, 4:8], ALU.add)
                S.act(ld4[:], ld4[:], AF.Exp)
                S.act(ld4[:], ld4[:], AF.Ln, bias=1.0)
                S.tt(ld4[:], ld4[:], negA[0:64, :], ALU.mult)
                S.mm(bank[0:64, 128:132], inclT, ld4[:])
                S.copy(g4[:], bank[0:64, 128:132], eng="act")
                S.ts(ng4[:], g4[:], -1.0, ALU.mult)
                S.act(eg4[:], g4[:], AF.Exp)
                S.tt(bg4[:], eg4[:], beta4[:], ALU.mult)
                S.tt(diag[:], id64.rearrange("p (h t) -> p h t", h=1).to_broadcast([64, 4, 64]),
                     g4[:].rearrange("p (h o) -> p h o", o=1).to_broadcast([64, 4, 64]), ALU.mult)
                bank = PS[5]
                S.mm(bank[:, 0:256], ones[0:64, :], diag[:].rearrange("p h t -> p (h t)"))
                S.copy(gbc[:].rearrange("p h t -> p (h t)"), bank[:, 0:256], eng="act")
                S.act(egbc[:].rearrange("p h t -> p (h t)"), bank[:, 0:256], AF.Exp)
                S.tt(dk4[:], gbc[0:64, :, C - 1], g4[:], ALU.subtract)
                S.act(dk4[:], dk4[:], AF.Exp)

                def gdn_gen(i):
                    kTc = fm[CK + i][:, cs]; qTc = fm[CQ + i][:, cs]; vTc = fm[CV + i][:, cs]
                    pb = PS[i]
                    S.tr(pb[0:64, 0:128], kTc, ident[:]); yield
                    S.tr(pb[0:64, 128:256], vTc, ident[:]); yield
                    S.ts(kdec[i][:], pb[0:64, 0:128], dk4[:, i:i + 1], ALU.mult); yield
                    S.ts(kbg[i][:], pb[0:64, 0:128], bg4[:, i:i + 1], ALU.mult); yield
                    S.ts(vbt[i][:], pb[0:64, 128:256], beta4[:, i:i + 1], ALU.mult); yield
                    S.stt(tA[i][:], gbc[0:64, i, :], -1.0, negm, ALU.mult, ALU.add); yield
                    S.act(dec[i][:], tA[i][:], AF.Exp, bias=g4[:, i:i + 1]); yield
                    S.tt(tB[i][:], gbc[0:64, i, :], negmT, ALU.add); yield
                    S.act(decT[i][:], tB[i][:], AF.Exp, bias=ng4[:, i:i + 1]); yield
                    S.mm(pb[0:64, 256:320], kTc, kTc); yield
                    S.mm(pb[0:64, 320:384], kTc, qTc); yield
                    S.stt(Nm[i][:], pb[0:64, 256:320], nbeta4[:, i:i + 1], dec[i][:], ALU.mult, ALU.mult); yield
                    S.tt(Nm[i][:], Nm[i][:], strict, ALU.mult); yield
                    S.tt(attnT[i][:], pb[0:64, 320:384], decT[i][:], ALU.mult); yield
                    S.tr(pb[0:64, 384:448], Nm[i][:], id64); yield
                    S.copy(Mm[i][:], pb[0:64, 384:448], eng="act"); yield
                    S.tt(X[i][:], Mm[i][:], id64, ALU.add); yield
                    Q, QT = Mm[i], Nm[i]
                    Qn, QTn = Q2[i], QT2[i]
                    for st in range(5):
                        S.mm(pb[0:64, 0:64], Q[:], QT[:]); yield
                        if st < 4:
                            S.mm(pb[0:64, 64:128], QT[:], Q[:]); yield
                        S.copy(QTn[:], pb[0:64, 0:64], eng="act"); yield
                        if st < 4:
                            S.copy(Qn[:], pb[0:64, 64:128], eng="dve"); yield
                        S.mm(pb[0:64, 128:192], QTn[:], X[i][:]); yield
                        S.tt(X[i][:], X[i][:], pb[0:64, 128:192], ALU.add); yield
                        Q, QT, Qn, QTn = Qn, QTn, Q, QT
                    S.mm(pb[0:64, 192:320], X[i][:], vbt[i][:]); yield
                    S.mm(pb[:, 320:384], kbg[i][:], X[i][:]); yield
                    S.copy(u[i][:], pb[0:64, 192:320], eng="act"); yield
                    S.copy(wT[i][:], pb[:, 320:384], eng="dve"); yield
                    S.tt(qgT[i][:], qTc, egbc[:, i, :], ALU.mult); yield
                    S.mm(pb[0:64, 0:128], wT[i][:], Sg[i][:]); yield
                    S.tt(vnew[i][:], u[i][:], pb[0:64, 0:128], ALU.subtract); yield
                    S.mm(pb[0:64, 128:256], qgT[i][:], Sg[i][:]); yield
                    S.copy(o1[i][:], pb[0:64, 128:256], eng="act"); yield
                    S.mm(pb[0:64, 256:384], attnT[i][:], vnew[i][:]); yield
                    S.mm(pb[:, 384:512], kdec[i][:], vnew[i][:]); yield
                    S.tt(o1[i][:], o1[i][:], pb[0:64, 256:384], ALU.add); yield
                    S.stt(Sg[i][:], Sg[i][:], egbc[:, i, C - 1:C], pb[:, 384:512], ALU.mult, ALU.add); yield
                    S.act(ojunk[i][:], o1[i][:], AF.Square, accum_out=oss[:, i:i + 1]); yield
                    S.ts(oss[:, i:i + 1], oss[:, i:i + 1], 1.0 / 128, ALU.mult, EPS, ALU.add); yield
                    S.act(oss[:, i:i + 1], oss[:, i:i + 1], AF.Sqrt); yield
                    S.recip(ors[:, i:i + 1], oss[:, i:i + 1]); yield
                    S.stt(o1[i][:], o1[i][:], ors[:, i:i + 1], gn[0:64, 0:128], ALU.mult, ALU.mult); yield
                    S.tr(pb[0:64, 0:128], fm[CZ + i][:, cs], ident[:]); yield
                    S.act(sz[i][:], pb[0:64, 0:128], AF.Silu); yield
                    S.tt(osb[par][:, i * 128:(i + 1) * 128], o1[i][:], sz[i][:], ALU.mult); yield

                def gla_gen(i):
                    pb = PS[4 + i]
                    S.mm(pb[0:64, 0:64], kb[i][:, cs], qb[i][:, cs]); yield
                    S.tt(lattn[i][:], pb[0:64, 0:64], inclT, ALU.mult); yield
                    S.tr(pb[0:64, 64:192], fm[LV + i][:, cs], ident[:]); yield
                    S.copy(lvtok[i][:], pb[0:64, 64:192], eng="act"); yield
                    S.tr(pb[0:64, 192:256], kdT[i][:, cs], id64); yield
                    S.copy(lkd[i][:], pb[0:64, 192:256], eng="dve"); yield
                    S.mm(pb[0:64, 384:512], qb[i][:, cs], Sl[i][:]); yield
                    S.copy(lo1[i][:], pb[0:64, 384:512], eng="act"); yield
                    S.mm(pb[0:64, 0:128], lattn[i][:], lvtok[i][:]); yield
                    S.tt(lo1[i][:], lo1[i][:], pb[0:64, 0:128], ALU.add); yield
                    S.mm(pb[0:64, 128:256], lkd[i][:], lvtok[i][:]); yield
                    S.stt(Sl[i][:], Sl[i][:], ebl[i][:, c:c + 1], pb[0:64, 128:256], ALU.mult, ALU.add); yield
                    S.act(ojunk[4 + i][:], lo1[i][:], AF.Square, accum_out=oss[:, 4 + i:5 + i]); yield
                    S.ts(oss[:, 4 + i:5 + i], oss[:, 4 + i:5 + i], 1.0 / 128, ALU.mult, EPS, ALU.add); yield
                    S.act(oss[:, 4 + i:5 + i], oss[:, 4 + i:5 + i], AF.Sqrt); yield
                    S.recip(ors[:, 4 + i:5 + i], oss[:, 4 + i:5 + i]); yield
                    S.stt(lo1[i][:], lo1[i][:], ors[:, 4 + i:5 + i], gn[0:64, 128:256], ALU.mult, ALU.mult); yield
                    S.tr(pb[0:64, 256:384], fm[LR + i][:, cs], ident[:]); yield
                    S.act(sz[4 + i][:], pb[0:64, 256:384], AF.Silu); yield
                    S.tt(lsb[par][:, i * 128:(i + 1) * 128], lo1[i][:], sz[4 + i][:], ALU.mult); yield

                gens = [gdn_gen(i) for i in range(4)] + [gla_gen(i) for i in range(4)]
                while gens:
                    for gnr in list(gens):
                        try:
                            next(gnr)
                        except StopIteration:
                            gens.remove(gnr)
                io["store_og"](tok0, osb[par][:])
                io["store_ol"](tok0, lsb[par][:])
        S.barrier()


def build_A(NSEQ=4096, stop=99):
    TB = 256
    nc = bass.Bass("TRN2", target_bir_lowering=False)

    def din(name, shape, dt=F32):
        return nc.dram_tensor(name, list(shape), dt, kind="ExternalInput").ap()

    x_d = din("x", [NSEQ, 4096])
    io = dict(wa=din("wa", [NCHUNKS_A, 128, 4096]), vecs=din("vecs", [128, 32]), cw=din("cw", [128, 12, 4]),
              cst=din("cst", [64, 256]), ident=din("ident", [128, 128]), hv=din("hv", [128, 12]), gn=din("gn", [128, 256]),
              wa2=din("wa2", [16, 4, 64]), ba=din("ba", [64, 4]), rmask=din("rmask", [64, TB]))
    og_d = nc.dram_tensor("og", [NSEQ, 512], F32, kind="ExternalOutput").ap()
    ol_d = nc.dram_tensor("ol", [NSEQ, 512], F32, kind="ExternalOutput").ap()
    with ExitStack() as es0:
        S = Sched(nc, es0)
        PS = [S.tile(f"ps{i}", [128, 512], F32, "psum") for i in range(8)]
        io["load_x"] = lambda xv, t: S.dma("sp", xv, x_d[t * 128:(t + 1) * 128, :])
        io["store_og"] = lambda tok0, v: S.dma("pool", og_d[tok0:tok0 + C, :], v)
        io["store_ol"] = lambda tok0, v: S.dma("pool", ol_d[tok0:tok0 + C, :], v)
        emit_A(S, PS, io, NSEQ=NSEQ, stop=stop)
        S.finish("pool")
        print("A instrs", S.ninstr, "sems", S.nsem)
    return nc


def _a_inputs(inp, L, g):
    hs = [4 * g + i for i in range(4)]
    cols = a_weight_cols(g)
    conv = inp["gdn_conv"][L]
    cw = np.stack([conv[:, cols[f]].T for f in range(12)], axis=1).astype(np.float32)
    hv = np.zeros((128, 12), np.float32)
    hv[:, 0:4] = inp["gdn_a_log"][L][hs][None]
    hv[:, 4:8] = inp["gdn_dt_bias"][L][hs][None]
    gn = np.concatenate([np.tile(inp["gdn_norm"][L][None], (128, 1)),
                         np.tile(inp["gla_norm"][L][None], (128, 1))], axis=1).astype(np.float32)
    wa2 = np.stack([inp["gla_w_a2"][L][:, h * 64:(h + 1) * 64] for h in hs], axis=1).astype(np.float32)
    ba = np.stack([inp["gla_b_a"][L][h * 64:(h + 1) * 64] for h in hs], axis=1).astype(np.float32)
    rm = np.ones((64, 256), np.float32)
    rm[:, ::64] = 0.0
    return dict(wa=a_tile_weights(inp["w_in"][L], g),
                vecs=np.ascontiguousarray(inp["norm_mix"][L].reshape(32, 128).T), cw=np.ascontiguousarray(cw),
                cst=a_consts(), ident=np.eye(128, dtype=np.float32), hv=hv, gn=gn,
                wa2=np.ascontiguousarray(wa2), ba=np.ascontiguousarray(ba), rmask=rm)


def _b_weights(inp, L):
    def vec(v):
        return np.ascontiguousarray(v.reshape(32, 128).T)
    w_in = inp["w_in"][L]
    return dict(
        wg=tile_w(w_in[:, 14384:]), wog=tile_w(inp["w_o_gdn"][L]), wol=tile_w(inp["w_o_gla"][L]),
        wout=tile_w(inp["w_out"][L]), wq=tile_w(inp["peer_w_query"][L]), wpg=tile_w(inp["w_ple_gate"][L]),
        wple=tile_w(inp["w_ple"][L]),
        kT=np.ascontiguousarray(np.stack([inp["peer_keys1"][L].T, inp["peer_keys2"][L].T]).astype(np.float32)),
        pu=np.ascontiguousarray(inp["peer_u"][L]), pv=np.ascontiguousarray(inp["peer_v"][L]),
        vecs=np.ascontiguousarray(np.concatenate([vec(inp["norm_mix"][L]), vec(inp["norm_ffn"][L]), vec(inp["norm_ple"][L])], axis=1)),
        nffn=np.ascontiguousarray(inp["norm_ffn"][L]), nfin=np.ascontiguousarray(inp["norm_final"]),
        ident=np.eye(128, dtype=np.float32), iota16=np.tile(np.arange(16, dtype=np.float32), (128, 1)),
    )


def kernel(**inp):
    inp = {k: np.asarray(v) for k, v in inp.items()}
    NCORE = 8
    x = np.ascontiguousarray(inp["x"], dtype=np.float32)
    Bn, Sn, Dn = x.shape
    ncA = build_A(NSEQ=Sn)
    ncB = {False: build_B(final=False), True: None}
    for L in range(2):
        ga = [_a_inputs(inp, L, g) for g in range(4)]
        in_maps = []
        for c in range(NCORE):
            b, g = divmod(c, 4)
            m = dict(ga[g])
            m["x"] = np.ascontiguousarray(x[b])
            in_maps.append(m)
        resA = run_bass_kernel_spmd(ncA, in_maps, core_ids=list(range(NCORE))).results
        del ga, in_maps
        og = np.empty((Bn, Sn, 2048), np.float32)
        ol = np.empty((Bn, Sn, 2048), np.float32)
        for c in range(NCORE):
            b, g = divmod(c, 4)
            og[b, :, g * 512:(g + 1) * 512] = resA[c]["og"]
            ol[b, :, g * 512:(g + 1) * 512] = resA[c]["ol"]
        final = (L == 1)
        if ncB.get(final) is None:
            ncB[final] = build_B(final=final)
        wB = _b_weights(inp, L)
        xf = x.reshape(Bn * Sn, Dn)
        ogf = og.reshape(Bn * Sn, 2048)
        olf = ol.reshape(Bn * Sn, 2048)
        pf = np.ascontiguousarray(inp["p"][L]).reshape(Bn * Sn, 256)
        in_maps = []
        for c in range(NCORE):
            sl = slice(c * 1024, (c + 1) * 1024)
            m = dict(wB)
            m.update(x=np.ascontiguousarray(xf[sl]), og=np.ascontiguousarray(ogf[sl]), ol=np.ascontiguousarray(olf[sl]),
                     p=np.ascontiguousarray(pf[sl]))
            in_maps.append(m)
        resB = run_bass_kernel_spmd(ncB[final], in_maps, core_ids=list(range(NCORE))).results
        del in_maps, wB
        x = np.concatenate([resB[c]["xo"] for c in range(NCORE)], axis=0).reshape(Bn, Sn, Dn)
    return np.ascontiguousarray(x, dtype=np.float32)


def build_fused():
    NSEQ, NTOK = 4096, 1024
    nc = bass.Bass("TRN2", target_bir_lowering=False)

    def din(name, shape, dt=F32):
        return nc.dram_tensor(name, list(shape), dt, kind="ExternalInput").ap()

    xseq_d = din("xseq", [NSEQ, 4096])
    xtok_d = din("xtok", [NTOK, 4096])
    p_d = din("p", [2, NTOK, 256])
    A_in = dict(wa=din("wa", [2, NCHUNKS_A, 128, 4096]), vecs=din("vecsA", [2, 128, 32]), cw=din("cw", [2, 128, 12, 4]),
                hv=din("hv", [2, 128, 12]), gn=din("gn", [2, 128, 256]), wa2=din("wa2", [2, 16, 4, 64]), ba=din("ba", [2, 64, 4]))
    cst_d = din("cst", [64, 256]); ident_d = din("ident", [128, 128]); rmask_d = din("rmask", [64, 256])
    B_in = dict(wg=din("wg", [2, 64, 128, 4096]), wog=din("wog", [2, 32, 128, 2048]), wol=din("wol", [2, 32, 128, 2048]),
                wout=din("wout", [2, 32, 128, 4096]), wq=din("wq", [2, 16, 128, 4096]), wpg=din("wpg", [2, 32, 128, 4096]),
                wple=din("wple", [2, 32, 128, 256]), kT=din("kT", [2, 2, 128, 128]),
                vecs=din("vecsB", [2, 128, 96]), nffn=din("nffn", [2, 4096]))
    nfin_d = din("nfin", [4096]); iota_d = din("iota16", [128, 16])
    pu_d = [din(f"pu{L}", [16384, 4096]) for L in range(2)]
    pv_d = [din(f"pv{L}", [16384, 4096]) for L in range(2)]
    xidx_d = din("xidx", [128, 32], I32)
    oidx_d = din("oidx", [128, 8, 4], I32)
    out_d = nc.dram_tensor("xo", [NTOK, 4096], F32, kind="ExternalOutput").ap()

    with ExitStack() as es0:
        S = Sched(nc, es0)
        PS = [S.tile(f"ps{i}", [128, 512], F32, "psum") for i in range(8)]
        oa = S.tile("oa_i", [NSEQ, 1024], F32, "dram")
        OA = S.tile("OA_i", [8 * NSEQ, 1024], F32, "dram")
        xb = S.tile("xb_i", [NTOK, 4096], F32, "dram")
        XG = S.tile("XG_i", [8 * NTOK, 4096], F32, "dram")
        xidx = S.tile("xidx_s", [128, 32], I32)
        oidx = S.tile("oidx_s", [128, 8, 4], I32)
        S.dma("sp", xidx[:], xidx_d[:, :])
        S.dma("sp", oidx[:], oidx_d[:, :, :])

        for L in range(2):
            ioA = {k: v[L] for k, v in A_in.items()}
            ioA.update(cst=cst_d, ident=ident_d, rmask=rmask_d)
            if L == 0:
                ioA["load_x"] = lambda xv, t: S.dma("sp", xv, xseq_d[t * 128:(t + 1) * 128, :])
            else:
                ioA["load_x"] = lambda xv, t: S.dma("pool", xv, XG[:, :], indirect=xidx[:, t:t + 1])

            def store_o(tok0, v, col0):
                dst = oa.sub((col0, tok0), (slice(tok0, tok0 + C), slice(col0, col0 + 512)))
                S.dma("pool", dst, v, owner_deps=_deps([v]))

            ioA["store_og"] = lambda tok0, v: store_o(tok0, v, 0)
            ioA["store_ol"] = lambda tok0, v: store_o(tok0, v, 512)
            emit_A(S, PS, ioA, NSEQ=NSEQ)
            S.collective("AllGather", View(oa[:, :].ap, tuple(oa.subs.values())), OA[:, :])

            ioB = {k: v[L] for k, v in B_in.items()}
            ioB.update(ident=ident_d, iota16=iota_d, nfin=nfin_d, pu=pu_d[L], pv=pv_d[L])

            def load_o(tmpx, t):
                for g in range(4):
                    S.dma("pool", tmpx[:, g * 1024:(g + 1) * 1024], OA[:, :], indirect=oidx[:, t, g:g + 1])
                return ((lambda k: tmpx[:, (k // 4) * 1024 + (k % 4) * 128:(k // 4) * 1024 + (k % 4 + 1) * 128]),
                        (lambda k: tmpx[:, (k // 4) * 1024 + 512 + (k % 4) * 128:(k // 4) * 1024 + 512 + (k % 4 + 1) * 128]))

            ioB["load_o"] = load_o
            if L == 0:
                ioB["load_x"] = lambda v, t: S.dma("sp", v, xtok_d[t * 128:(t + 1) * 128, :])
                ioB["store_x"] = lambda t, v: S.dma("pool", xb.sub(t, (slice(t * 128, (t + 1) * 128), slice(None))), v,
                                                    owner_deps=_deps([v]))
            else:
                ioB["load_x"] = lambda v, t: S.dma("sp", v, xb.sub(t, (slice(t * 128, (t + 1) * 128), slice(None))))
                ioB["store_x"] = lambda t, v: S.dma("pool", out_d[t * 128:(t + 1) * 128, :], v)
            ioB["load_p"] = lambda v, t, L=L: S.dma("sp", v, p_d[L, t * 128:(t + 1) * 128, :])
            emit_B(S, PS, ioB, final=(L == 1), NTOK=NTOK)
            if L == 0:
                S.collective("AllGather", View(xb[:, :].ap, tuple(xb.subs.values())), XG[:, :])
        S.finish("pool")
        print("fused instrs", S.ninstr, "sems", S.nsem)
    return nc


def kernel_fused_experimental(**inp):
    inp = {k: np.asarray(v) for k, v in inp.items()}
    NCORE = 8
    x = np.ascontiguousarray(inp["x"], dtype=np.float32)
    Bn, Sn, Dn = x.shape
    nc = build_fused()
    ga = [[_a_inputs(inp, L, g) for g in range(4)] for L in range(2)]
    wB = [_b_weights(inp, L) for L in range(2)]
    shared = {}
    for k in ("wg", "wog", "wol", "wout", "wq", "wpg", "wple", "kT", "nffn"):
        shared[k] = np.stack([wB[0][k], wB[1][k]])
    for L in range(2):
        shared[f"pu{L}"] = wB[L]["pu"]
        shared[f"pv{L}"] = wB[L]["pv"]
    shared["vecsB"] = np.stack([wB[0]["vecs"], wB[1]["vecs"]])
    shared["nfin"] = wB[0]["nfin"]
    shared["iota16"] = wB[0]["iota16"]
    shared["ident"] = wB[0]["ident"]
    shared["cst"] = ga[0][0]["cst"]
    shared["rmask"] = ga[0][0]["rmask"]
    del wB
    xf = x.reshape(Bn * Sn, Dn)
    in_maps = []
    pp = np.arange(128)
    for c in range(NCORE):
        b, g = divmod(c, 4)
        m = dict(shared)
        for k, kk in (("wa", "wa"), ("vecsA", "vecs"), ("cw", "cw"), ("hv", "hv"), ("gn", "gn"), ("wa2", "wa2"), ("ba", "ba")):
            m[k] = np.stack([ga[0][g][kk], ga[1][g][kk]])
        m["xseq"] = np.ascontiguousarray(x[b])
        m["xtok"] = np.ascontiguousarray(xf[c * 1024:(c + 1) * 1024])
        m["p"] = np.ascontiguousarray(np.stack([inp["p"][L].reshape(Bn * Sn, 256)[c * 1024:(c + 1) * 1024] for L in range(2)]))
        m["xidx"] = np.ascontiguousarray((b * Sn + np.arange(32)[None, :] * 128 + pp[:, None]).astype(np.int32))
        oi = ((b * 4 + np.arange(4))[None, None, :] * Sn + g * 1024 + np.arange(8)[None, :, None] * 128 + pp[:, None, None])
        m["oidx"] = np.ascontiguousarray(oi.astype(np.int32))
        in_maps.append(m)
    res = run_bass_kernel_spmd(nc, in_maps, core_ids=list(range(NCORE))).results
    out = np.concatenate([res[c]["xo"] for c in range(NCORE)], axis=0).reshape(Bn, Sn, Dn)
    return np.ascontiguousarray(out, dtype=np.float32)
```

```python
from contextlib import ExitStack
from concourse.bass_utils import run_bass_kernel_spmd
import numpy as np
import concourse.bass as bass
import concourse.mybir as mybir

F32 = mybir.dt.float32
BF16 = mybir.dt.bfloat16
I32 = mybir.dt.int32
U32 = mybir.dt.uint32
AF = mybir.ActivationFunctionType
ALU = mybir.AluOpType
AX = mybir.AxisListType

SEM_ROT = 30000


class _Sem:
    def __init__(self, h, eng=None):
        self.h = h
        self.total = 0
        self.eng = eng


class Dep:
    __slots__ = ("name", "last_w", "readers", "dsem", "excl")

    def __init__(self, name, excl=False):
        self.name = name
        self.last_w = None
        self.readers = {}
        self.dsem = None
        self.excl = excl


class View:
    __slots__ = ("ap", "deps")

    def __init__(self, ap, deps):
        self.ap = ap
        self.deps = deps

    def __getitem__(self, idx):
        return View(self.ap[idx], self.deps)

    def rearrange(self, *a, **k):
        return View(self.ap.rearrange(*a, **k), self.deps)

    def broadcast(self, *a, **k):
        return View(self.ap.broadcast(*a, **k), self.deps)

    def to_broadcast(self, *a, **k):
        return View(self.ap.to_broadcast(*a, **k), self.deps)

    def bitcast(self, *a, **k):
        return View(self.ap.bitcast(*a, **k), self.deps)

    def key(self, dep):
        return View(self.ap, (dep,))


class T:
    def __init__(self, S, name, shape, dtype, space="sbuf"):
        self.S = S
        S.ntile = getattr(S, "ntile", 0) + 1
        name = f"{name}_u{S.ntile}"
        self.name = name
        self.space = space
        if space == "sbuf":
            self.t = S.tes.enter_context(S.nc.sbuf_tensor(name, list(shape), dtype))
        elif space == "psum":
            self.t = S.tes.enter_context(S.nc.psum_tensor(name, list(shape), dtype))
        else:
            self.t = S.nc.dram_tensor(name, list(shape), dtype, kind="Internal")
        self.dep = Dep(name, excl=(space == "psum"))
        self.subs = {}

    def __getitem__(self, idx):
        if self.space == "dram":
            return View(self.t.ap()[idx], (self.dep,))
        return View(self.t[idx], (self.dep,))

    def sub(self, key, idx):
        d = self.subs.get(key)
        if d is None:
            d = self.subs[key] = Dep(f"{self.name}.{key}")
        v = self[idx]
        return View(v.ap, (d,))


def _ap(x):
    return x.ap if isinstance(x, View) else x


def _deps(xs):
    out = []
    for x in xs:
        if isinstance(x, View):
            out.extend(x.deps)
    return out


class Sched:
    def __init__(self, nc, es):
        self.nc = nc
        self.es = es
        self.tes = es
        self.engs = {"pe": nc.tensor, "act": nc.scalar, "dve": nc.vector, "pool": nc.gpsimd, "sp": nc.sync}
        self.cur = {}
        self.waited = {e: {} for e in self.engs}
        self.nsem = 0
        self.all_dma_sems = []
        self.ninstr = 0
        for e in self.engs:
            self._rot(e)

    def _newsem(self, name, eng=None):
        self.nsem += 1
        return _Sem(self.es.enter_context(self.nc.semaphore(f"{name}_{self.nsem}")), eng)

    def _rot(self, e):
        self.cur[e] = self._newsem("e" + e, e)

    def tile(self, name, shape, dtype, space="sbuf"):
        return T(self, name, shape, dtype, space)

    def _wait(self, e, toks):
        best = {}
        for (s, v) in toks:
            if e == "pe" and s.eng == "pe":
                continue
            if best.get(s, 0) < v:
                best[s] = v
        w = self.waited[e]
        for s, v in best.items():
            if w.get(s, 0) < v:
                self.engs[e].wait_ge(s.h, v)
                w[s] = v

    def _collect(self, reads, writes):
        toks = []
        for d in reads:
            if d.last_w is not None:
                toks.append(d.last_w)
        for d in writes:
            if d.last_w is not None:
                toks.append(d.last_w)
            toks.extend(d.readers.items())
        return toks

    def _commit(self, tok, reads, writes):
        s, v = tok
        for d in reads:
            if d.readers.get(s, 0) < v:
                d.readers[s] = v
        for d in writes:
            d.last_w = tok
            d.readers = {}

    def op(self, e, fn, reads=(), writes=()):
        rd = _deps(reads)
        wd = _deps(writes)
        wd = wd + [d for d in rd if d.excl]
        self._wait(e, self._collect(rd, wd))
        ins = fn(self.engs[e])
        sem = self.cur[e]
        sem.total += 1
        ins.then_inc(sem.h, 1)
        tok = (sem, sem.total)
        self._commit(tok, rd, wd)
        self.ninstr += 1
        if sem.total >= SEM_ROT:
            self._rot(e)
        return tok

    def dma(self, q, out, in_, indirect=None, **kw):
        rd = _deps([in_] + ([indirect] if indirect is not None else []))
        wd = _deps([out])
        owner = None
        for d in (kw.pop("owner_deps", None) or (wd + rd)):
            owner = d
            break
        assert owner is not None
        if owner.dsem is None:
            owner.dsem = self._newsem("d")
            self.all_dma_sems.append(owner.dsem)
        sem = owner.dsem
        toks = self._collect(rd, wd)
        if sem.total:
            toks.append((sem, sem.total))
        self._wait(q, toks)
        eng = self.engs[q]
        if indirect is not None:
            ins = eng.indirect_dma_start(
                out=_ap(out), out_offset=None, in_=_ap(in_),
                in_offset=bass.IndirectOffsetOnAxis(ap=_ap(indirect), axis=0), **kw)
        else:
            ins = eng.dma_start(out=_ap(out), in_=_ap(in_), **kw)
        sem.total += 16
        ins.then_inc(sem.h, 16)
        tok = (sem, sem.total)
        self._commit(tok, rd, wd)
        self.ninstr += 1
        return tok

    def barrier(self):
        toks = [(s, s.total) for s in self.all_dma_sems if s.total]
        toks += [(s, s.total) for s in self.cur.values() if s.total]
        for e in self.engs:
            self._wait(e, [t for t in toks if t[0].eng != e])

    def finish(self, e="sp"):
        toks = [(s, s.total) for s in self.all_dma_sems if s.total]
        self._wait(e, toks)

    def mm(self, out, lhsT, rhs, start=True, stop=True, **kw):
        return self.op("pe", lambda e: e.matmul(_ap(out), _ap(lhsT), _ap(rhs), start=start, stop=stop, **kw),
                       reads=[lhsT, rhs] + ([] if start else [out]), writes=[out])

    def tr(self, out, in_, ident):
        return self.op("pe", lambda e: e.transpose(_ap(out), _ap(in_), _ap(ident)),
                       reads=[in_, ident], writes=[out])

    def act(self, out, in_, func, bias=None, scale=None, accum_out=None, eng="act"):
        kw = {}
        rd = [in_]
        wr = [out]
        if bias is not None:
            kw["bias"] = _ap(bias)
            rd.append(bias)
        if scale is not None:
            kw["scale"] = _ap(scale)
            rd.append(scale)
        if accum_out is not None:
            kw["accum_out"] = _ap(accum_out)
            wr.append(accum_out)
        return self.op(eng, lambda e: e.activation(_ap(out), _ap(in_), func, **kw), reads=rd, writes=wr)

    def tt(self, out, in0, in1, op, eng="dve"):
        return self.op(eng, lambda e: e.tensor_tensor(_ap(out), _ap(in0), _ap(in1), op),
                       reads=[in0, in1], writes=[out])

    def ts(self, out, in0, s1, op0, s2=None, op1=None, accum_out=None, eng="dve"):
        kw = {}
        wr = [out]
        if op1 is not None:
            kw["op1"] = op1
        if accum_out is not None:
            kw["accum_out"] = _ap(accum_out)
            wr.append(accum_out)
        return self.op(eng, lambda e: e.tensor_scalar(_ap(out), _ap(in0), _ap(s1), _ap(s2) if s2 is not None else None,
                                                      op0, **kw),
                       reads=[in0, s1, s2], writes=wr)

    def stt(self, out, in0, scalar, in1, op0, op1, accum_out=None, eng="dve"):
        kw = {}
        wr = [out]
        if accum_out is not None:
            kw["accum_out"] = _ap(accum_out)
            wr.append(accum_out)
        return self.op(eng, lambda e: e.scalar_tensor_tensor(_ap(out), _ap(in0), _ap(scalar), _ap(in1), op0, op1, **kw),
                       reads=[in0, scalar, in1], writes=wr)

    def copy(self, out, in_, eng="dve"):
        if eng == "act":
            return self.op("act", lambda e: e.copy(_ap(out), _ap(in_)), reads=[in_], writes=[out])
        return self.op(eng, lambda e: e.tensor_copy(_ap(out), _ap(in_)), reads=[in_], writes=[out])

    def memset(self, out, val, eng="dve"):
        return self.op(eng, lambda e: e.memset(_ap(out), val), reads=[], writes=[out])

    def recip(self, out, in_):
        return self.op("dve", lambda e: e.reciprocal(_ap(out), _ap(in_)), reads=[in_], writes=[out])

    def reduce(self, out, in_, op, axis=AX.X):
        return self.op("dve", lambda e: e.tensor_reduce(_ap(out), _ap(in_), axis, op), reads=[in_], writes=[out])


class Arena:
    def __init__(self, S, name, nslab, slab_f32=4096):
        self.T = S.tile(name, [128, nslab * slab_f32], F32)
        self.deps = [Dep(f"{name}{i}") for i in range(nslab)]
        self.sf = slab_f32

    def view(self, s0, ns=1, dtype=None):
        ap = self.T.t[:, s0 * self.sf:(s0 + ns) * self.sf]
        if dtype is not None and dtype != F32:
            ap = ap.bitcast(dtype)
        return View(ap, tuple(self.deps[s0:s0 + ns]))


CC_INC = 1


def _sched_collective(self, kind, src, dst, ngroup=8):
    rd = _deps([src])
    wd = _deps([dst])
    owner = wd[0]
    if owner.dsem is None:
        owner.dsem = self._newsem("c")
        self.all_dma_sems.append(owner.dsem)
    sem = owner.dsem
    toks = self._collect(rd, wd)
    if sem.total:
        toks.append((sem, sem.total))
    self._wait("pool", toks)
    ins = self.nc.gpsimd.collective_compute(kind, ALU.bypass, replica_groups=[list(range(ngroup))],
                                            ins=[_ap(src).opt()], outs=[_ap(dst).opt()])
    sem.total += CC_INC
    ins.then_inc(sem.h, CC_INC)
    tok = (sem, sem.total)
    self._commit(tok, rd, wd)
    self.ninstr += 1
    return tok


Sched.collective = _sched_collective


EPS = 1e-6
GELU_C = 1.5957691216057308


def tile_w(W):
    K, N = W.shape
    return np.ascontiguousarray(W.reshape(K // 128, 128, N // 128, 128).transpose(2, 1, 0, 3)).reshape(N // 128, 128, K)


def emit_B(S, PS, io, final, NTOK=1024, do_peer=True, n_slots=128):
    TB = 256
    TPB = TB // 128
    NB = NTOK // TB
    with ExitStack() as es:
        S.tes = es
        AR = Arena(S, "ar", 10)
        xres = AR.view(0, 2).rearrange("p (t d) -> p t d", t=TPB)
        hT = AR.view(2, 1, BF16).rearrange("p (k t) -> p k t", k=32)
        eq = AR.view(2, 1)[:, 0:2048].rearrange("p (h k a) -> p h k a", h=8, k=16)
        tmpx = AR.view(3, 1)
        cand = AR.view(3, 1)[:, 0:2048]
        candw = AR.view(3, 1)[:, 2048:4096]
        nbc = AR.view(4, 1)
        stage = [AR.view(5 + i, 1) for i in range(3)]
        ubuf = [AR.view(5 + i, 1, BF16)[:, 0:4096] for i in range(3)]
        djunk = AR.view(2, 1)
        s8b = AR.view(8, 1, BF16)
        ogT = s8b[:, 0:4096].rearrange("p (k t) -> p k t", k=16)
        olT = s8b[:, 4096:8192].rearrange("p (k t) -> p k t", k=16)
        sc = AR.view(8, 1)[:, 0:2048]
        scw = AR.view(8, 1)[:, 2048:4096]
        acc = AR.view(8, 1)
        mT = AR.view(9, 1, BF16).rearrange("p (k t) -> p k t", k=32)
        qT = AR.view(9, 1).rearrange("p (k t) -> p k t", k=16)
        wb = [S.tile(f"wb{i}", [128, 4096], BF16) for i in range(3)]
        ident = S.tile("ident_s", [128, 128], F32)
        iota16 = S.tile("iota16_s", [128, 16], F32)
        vecs = S.tile("vecs_s", [128, 96], F32)
        kT = S.tile("kT_s", [128, 2, 128], F32)
        pT = S.tile("pT", [128, 2, TB], BF16)
        pin = S.tile("pin", [128, 256], F32)
        t1 = [S.tile(f"t1_{i}", [128, TB], F32) for i in range(2)]
        t2 = [S.tile(f"t2_{i}", [128, TB], F32) for i in range(2)]
        dxT = [S.tile(f"dxT{i}", [128, TB], F32) for i in range(2)]
        ss = S.tile("ss", [128, 4], F32)
        rs = S.tile("rs", [128, 4], F32)
        v16 = S.tile("v16", [128, 16, 16], F32)
        i16 = S.tile("i16", [128, 16, 16], U32)
        i16f = S.tile("i16f", [128, 16, 16], F32)
        tv = S.tile("tv", [128, 8, 16], F32)
        pos = S.tile("pos", [128, 8, 16], U32)
        pa = S.tile("pa", [128, 8, 16], U32)
        pb = S.tile("pb", [128, 8, 16], U32)
        paf = S.tile("paf", [128, 8, 16], F32)
        pbf = S.tile("pbf", [128, 8, 16], F32)
        sel1 = S.tile("sel1", [128, 8, 16], F32)
        sel2 = S.tile("sel2", [128, 8, 16], F32)
        idxf = S.tile("idxf", [128, 128], F32)
        idxi = S.tile("idxi", [128, 128], I32)
        gate = S.tile("gate", [128, 8, 16], F32)
        gsum = S.tile("gsum", [128, 8], F32)
        dots = S.tile("dots", [128, 128], F32)
        ga = S.tile("ga", [128, 128], F32)
        gb = S.tile("gb", [128, 128], F32)
        identb = S.tile("identb", [128, 128], BF16)
        dg = [S.tile(f"dg{k}", [128, 128], BF16) for k in range(4)]

        S.dma("sp", ident[:], io["ident"][:, :])
        S.copy(identb[:], ident[:])
        S.dma("sp", iota16[:], io["iota16"][:, :])
        S.dma("sp", vecs[:], io["vecs"][:, :])
        S.dma("sp", kT[:], io["kT"].rearrange("c p k -> p c k"))

        wcnt = [0]

        def getw(w_blk, K):
            i = wcnt[0] % 3
            wcnt[0] += 1
            S.dma("sp", stage[i][:, 0:K], w_blk)
            S.copy(wb[i][:, 0:K], stage[i][:, 0:K], eng=("pool" if wcnt[0] % 2 else "act"))
            return wb[i]

        trc = [0]

        def to_feat(src, dst, tok_off, nk, wcol=None):
            for g in range(0, nk, 4):
                n = min(4, nk - g)
                bank = PS[trc[0] % 2]
                trc[0] += 1
                for j in range(n):
                    S.tr(bank[:, j * 128:(j + 1) * 128], (src(g + j) if callable(src) else src[:, (g + j) * 128:(g + j + 1) * 128]), ident[:])
                o = dst[:, g:g + n, tok_off:tok_off + 128]
                i_ = bank[:, 0:n * 128].rearrange("p (k t) -> p k t", k=n)
                if wcol is not None:
                    w_ = wcol[:, g:g + n].rearrange("p (k o) -> p k o", o=1).to_broadcast([128, n, 128])
                    S.tt(o, i_, w_, ALU.mult)
                else:
                    S.copy(o, i_, eng=("act" if (trc[0] % 2) else "dve"))

        def rstd_of(xv, col):
            S.act(tmpx, xv, AF.Square, accum_out=ss[:, col:col + 1])
            S.ts(ss[:, col:col + 1], ss[:, col:col + 1], 1.0 / 4096, ALU.mult, EPS, ALU.add)
            S.act(ss[:, col:col + 1], ss[:, col:col + 1], AF.Sqrt)
            S.recip(rs[:, col:col + 1], ss[:, col:col + 1])

        def norm_to_hT(wcol):
            for t in range(TPB):
                rstd_of(xres[:, t, :], t)
                S.ts(tmpx, xres[:, t, :], rs[:, t:t + 1], ALU.mult)
                to_feat(tmpx, hT, t * 128, 32, wcol)

        def back_add(f, src):
            bank = PS[4 + f % 2]
            for t in range(TPB):
                S.tr(bank[:, t * 128:(t + 1) * 128], src[:, t * 128:(t + 1) * 128], ident[:])
            xv = xres[:, :, f * 128:(f + 1) * 128]
            S.tt(xv, xv, bank[:, 0:TPB * 128].rearrange("p (t c) -> p t c", t=TPB), ALU.add)

        def vmax(o, i_):
            S.op("dve", lambda e: e.max(_ap(o), _ap(i_)), reads=[i_], writes=[o])

        def vmaxidx(o, m, i_):
            S.op("dve", lambda e: e.max_index(_ap(o), _ap(m), _ap(i_)), reads=[m, i_], writes=[o])

        def vrepl(o, m, i_):
            S.op("dve", lambda e: e.match_replace(_ap(o), _ap(m), _ap(i_), -1e30), reads=[m, i_], writes=[o])

        def top16(vals, idxs, src, work):
            vmax(vals[:, 0:8], src)
            vmaxidx(idxs[:, 0:8], vals[:, 0:8], src)
            vrepl(work, vals[:, 0:8], src)
            vmax(vals[:, 8:16], work)
            vmaxidx(idxs[:, 8:16], vals[:, 8:16], work)

        if do_peer:
            pub = S.tile("pub_i", [16384, 4096], BF16, "dram")
            pvb = S.tile("pvb_i", [16384, 4096], BF16, "dram")
            cc = 0
            for (src_t, dst_t) in ((io["pu"], pub), (io["pv"], pvb)):
                for r in range(128):
                    i = cc % 3
                    S.dma("sp", stage[i], src_t[r * 128:(r + 1) * 128, :])
                    S.copy(wb[i][:], stage[i], eng=("act", "dve", "pool")[cc % 3])
                    S.dma("pool", dst_t.sub(r, (slice(r * 128, (r + 1) * 128), slice(None))), wb[i][:],
                          owner_deps=_deps([wb[i][:]]))
                    cc += 1
            S.barrier()
        for blk in range(NB):
            r0 = blk * TB
            for t in range(TPB):
                io["load_x"](xres[:, t, :], blk * TPB + t)
            norm_to_hT(vecs[:, 0:32])
            for t in range(TPB):
                ogc, olc = io["load_o"](tmpx, blk * TPB + t)
                to_feat(ogc, ogT, t * 128, 16)
                to_feat(olc, olT, t * 128, 16)
            for f in range(32):
                b0 = (f % 2) * 4
                w = getw(io["wog"][f], 2048)
                for kc in range(16):
                    S.mm(PS[b0][:, 0:TB], w[:, kc * 128:(kc + 1) * 128], ogT[:, kc, :], start=(kc == 0), stop=(kc == 15))
                w = getw(io["wol"][f], 2048)
                for kc in range(16):
                    S.mm(PS[b0 + 1][:, 0:TB], w[:, kc * 128:(kc + 1) * 128], olT[:, kc, :], start=(kc == 0), stop=(kc == 15))
                w = getw(io["wg"][f], 4096)
                for kc in range(32):
                    S.mm(PS[b0 + 2][:, 0:TB], w[:, kc * 128:(kc + 1) * 128], hT[:, kc, :], start=(kc == 0), stop=(kc == 31))
                w = getw(io["wg"][32 + f], 4096)
                for kc in range(32):
                    S.mm(PS[b0 + 3][:, 0:TB], w[:, kc * 128:(kc + 1) * 128], hT[:, kc, :], start=(kc == 0), stop=(kc == 31))
                a1 = t1[f % 2]; a2 = t2[f % 2]
                S.act(a1[:], PS[b0 + 2][:, 0:TB], AF.Sigmoid)
                S.act(a2[:], PS[b0 + 3][:, 0:TB], AF.Sigmoid)
                S.tt(a1[:], a1[:], PS[b0][:, 0:TB], ALU.mult)
                S.tt(a2[:], a2[:], PS[b0 + 1][:, 0:TB], ALU.mult)
                S.tt(mT[:, f, :], a1[:], a2[:], ALU.add, eng="pool")
            for f in range(32):
                w = getw(io["wout"][f], 4096)
                for kc in range(32):
                    S.mm(PS[f % 2][:, 0:TB], w[:, kc * 128:(kc + 1) * 128], mT[:, kc, :], start=(kc == 0), stop=(kc == 31))
                S.copy(dxT[f % 2][:], PS[f % 2][:, 0:TB], eng="act")
                back_add(f, dxT[f % 2])
            if do_peer:
                S.dma("sp", nbc, io["nffn"].partition_broadcast(128))
                norm_to_hT(vecs[:, 32:64])
                for f in range(16):
                    w = getw(io["wq"][f], 4096)
                    for kc in range(32):
                        S.mm(PS[f % 2][:, 0:TB], w[:, kc * 128:(kc + 1) * 128], hT[:, kc, :], start=(kc == 0), stop=(kc == 31))
                    S.copy(qT[:, f, :], PS[f % 2][:, 0:TB], eng="act")
                for t in range(TPB):
                    for c in range(16):
                        S.mm(PS[4 + c // 4][:, (c % 4) * 128:(c % 4 + 1) * 128], qT[:, c, t * 128:(t + 1) * 128], kT[:, c % 2, :])
                    for b in range(4):
                        S.copy(sc[:, b * 512:(b + 1) * 512], PS[4 + b][:, 0:512], eng=("act" if b % 2 else "dve"))
                    for c in range(16):
                        top16(v16[:, c, :], i16[:, c, :], sc[:, c * 128:(c + 1) * 128], scw[:, c * 128:(c + 1) * 128])
                    S.copy(i16f[:], i16[:])
                    v4 = v16[:].rearrange("p (h c) k -> p h c k", c=2)
                    i4 = i16f[:].rearrange("p (h c) k -> p h c k", c=2)
                    cand4 = cand.rearrange("p (h a b) -> p h a b", h=8, a=16)
                    S.tt(cand4,
                         v4[:, :, 0, :].rearrange("p h (a o) -> p h a o", o=1).to_broadcast([128, 8, 16, 16]),
                         v4[:, :, 1, :].rearrange("p h (o b) -> p h o b", o=1).to_broadcast([128, 8, 16, 16]),
                         ALU.add)
                    for h in range(8):
                        top16(tv[:, h, :], pos[:, h, :], cand[:, h * 256:(h + 1) * 256], candw[:, h * 256:(h + 1) * 256])
                    S.ts(pa[:], pos[:], 4, ALU.logical_shift_right)
                    S.ts(pb[:], pos[:], 15, ALU.bitwise_and)
                    S.copy(paf[:], pa[:])
                    S.copy(pbf[:], pb[:])
                    io4 = iota16[:].rearrange("p (h k a) -> p h k a", h=1, k=1).to_broadcast([128, 8, 16, 16])
                    for (pf, half, sel) in ((paf, 0, sel1), (pbf, 1, sel2)):
                        S.tt(eq, pf[:].rearrange("p h (k o) -> p h k o", o=1).to_broadcast([128, 8, 16, 16]), io4, ALU.is_equal)
                        S.tt(eq, eq, i4[:, :, half, :].rearrange("p h (o a) -> p h o a", o=1).to_broadcast([128, 8, 16, 16]), ALU.mult)
                        S.reduce(sel[:], eq, ALU.add)
                    S.stt(idxf[:], sel1[:].rearrange("p h k -> p (h k)"), 128.0, sel2[:].rearrange("p h k -> p (h k)"), ALU.mult, ALU.add)
                    S.copy(idxi[:], idxf[:])
                    S.tt(gate[:], tv[:], tv[:, :, 0:1].to_broadcast([128, 8, 16]), ALU.subtract)
                    S.act(gate[:], gate[:], AF.Exp)
                    S.reduce(gsum[:], gate[:], ALU.add)
                    S.recip(gsum[:], gsum[:])
                    S.tt(gate[:], gate[:], gsum[:].rearrange("p (h o) -> p h o", o=1).to_broadcast([128, 8, 16]), ALU.mult)
                    hn = tmpx
                    S.stt(hn, xres[:, t, :], rs[:, t:t + 1], nbc, ALU.mult, ALU.mult)
                    for s in range(n_slots):
                        ub = ubuf[s % 3]
                        S.dma("pool", ub, pub[:, :], indirect=idxi[:, s:s + 1])
                        S.stt(djunk, ub, 1.0, hn, ALU.mult, ALU.mult, accum_out=dots[:, s:s + 1])
                    S.tt(ga[:], dots[:], dots[:], ALU.mult)
                    S.ts(ga[:], ga[:], 0.044715, ALU.mult, 1.0, ALU.add)
                    S.tt(ga[:], ga[:], dots[:], ALU.mult)
                    S.act(ga[:], ga[:], AF.Sigmoid, scale=GELU_C)
                    S.tt(ga[:], ga[:], dots[:], ALU.mult)
                    S.tt(gb[:], ga[:], gate[:].rearrange("p h k -> p (h k)"), ALU.mult)
                    for s in range(n_slots):
                        vb = ubuf[s % 3]
                        S.dma("pool", vb, pvb[:, :], indirect=idxi[:, s:s + 1])
                        dgt = dg[s % 4]
                        S.ts(dgt[:], identb[:], gb[:, s:s + 1], ALU.mult, eng=("pool" if s % 2 else "dve"))
                        for n in range(8):
                            S.mm(PS[n][:, 0:512], dgt[:], vb[:, n * 512:(n + 1) * 512], start=(s == 0), stop=(s == n_slots - 1))
                    for n in range(8):
                        xv = xres[:, t, n * 512:(n + 1) * 512]
                        S.tt(xv, xv, PS[n][:, 0:512], ALU.add)
            norm_to_hT(vecs[:, 64:96])
            for t in range(TPB):
                io["load_p"](pin[:], blk * TPB + t)
                to_feat(pin[:], pT, t * 128, 2)
            for f in range(32):
                w = getw(io["wpg"][f], 4096)
                for kc in range(32):
                    S.mm(PS[f % 2][:, 0:TB], w[:, kc * 128:(kc + 1) * 128], hT[:, kc, :], start=(kc == 0), stop=(kc == 31))
                w = getw(io["wple"][f], 256)
                for kc in range(2):
                    S.mm(PS[2 + f % 2][:, 0:TB], w[:, kc * 128:(kc + 1) * 128], pT[:, kc, :], start=(kc == 0), stop=(kc == 1))
                a1 = t1[f % 2]
                S.act(a1[:], PS[f % 2][:, 0:TB], AF.Sigmoid)
                S.tt(dxT[f % 2][:], a1[:], PS[2 + f % 2][:, 0:TB], ALU.mult)
                back_add(f, dxT[f % 2])
            if final:
                S.dma("sp", nbc, io["nfin"].partition_broadcast(128))
            for t in range(TPB):
                if final:
                    rstd_of(xres[:, t, :], t)
                    S.stt(tmpx, xres[:, t, :], rs[:, t:t + 1], nbc, ALU.mult, ALU.mult)
                    io["store_x"](blk * TPB + t, tmpx)
                else:
                    io["store_x"](blk * TPB + t, xres[:, t, :])
        S.barrier()


def build_B(final, NTOK=1024, do_peer=True, n_slots=128):
    nc = bass.Bass("TRN2", target_bir_lowering=False)

    def din(name, shape, dt=F32):
        return nc.dram_tensor(name, list(shape), dt, kind="ExternalInput").ap()

    x_d = din("x", [NTOK, 4096]); og_d = din("og", [NTOK, 2048]); ol_d = din("ol", [NTOK, 2048]); p_d = din("p", [NTOK, 256])
    io = dict(wg=din("wg", [64, 128, 4096]), wog=din("wog", [32, 128, 2048]), wol=din("wol", [32, 128, 2048]),
              wout=din("wout", [32, 128, 4096]), wq=din("wq", [16, 128, 4096]), wpg=din("wpg", [32, 128, 4096]),
              wple=din("wple", [32, 128, 256]), kT=din("kT", [2, 128, 128]),
              pu=din("pu", [16384, 4096]), pv=din("pv", [16384, 4096]),
              vecs=din("vecs", [128, 96]), nffn=din("nffn", [4096]), nfin=din("nfin", [4096]),
              ident=din("ident", [128, 128]), iota16=din("iota16", [128, 16]))
    out_d = nc.dram_tensor("xo", [NTOK, 4096], F32, kind="ExternalOutput").ap()
    with ExitStack() as es0:
        S = Sched(nc, es0)
        PS = [S.tile(f"ps{i}", [128, 512], F32, "psum") for i in range(8)]

        def load_o(tmpx, t):
            S.dma("sp", tmpx[:, 0:2048], og_d[t * 128:(t + 1) * 128, :])
            S.dma("sp", tmpx[:, 2048:4096], ol_d[t * 128:(t + 1) * 128, :])
            return (lambda k: tmpx[:, k * 128:(k + 1) * 128]), (lambda k: tmpx[:, 2048 + k * 128:2048 + (k + 1) * 128])

        io["load_x"] = lambda v, t: S.dma("sp", v, x_d[t * 128:(t + 1) * 128, :])
        io["load_o"] = load_o
        io["load_p"] = lambda v, t: S.dma("sp", v, p_d[t * 128:(t + 1) * 128, :])
        io["store_x"] = lambda t, v: S.dma("pool", out_d[t * 128:(t + 1) * 128, :], v)
        emit_B(S, PS, io, final, NTOK=NTOK, do_peer=do_peer, n_slots=n_slots)
        S.finish("pool")
        print("B instrs", S.ninstr, "sems", S.nsem)
    return nc


C = 64
NCHUNKS_A = 33
CQ, CK, CV, CZ, LQ, LK, LV, LR, CM = 0, 4, 8, 12, 16, 20, 24, 28, 32


def a_weight_cols(g):
    cols = []
    hs = [4 * g + i for i in range(4)]
    for base in (0, 2048, 4096, 6144):
        for h in hs:
            cols.append(np.arange(base + h * 128, base + (h + 1) * 128))
    for base in (8224, 9248):
        for h in hs:
            c = np.full(128, -1)
            c[:64] = np.arange(base + h * 64, base + (h + 1) * 64)
            cols.append(c)
    for base in (10272, 12320):
        for h in hs:
            cols.append(np.arange(base + h * 128, base + (h + 1) * 128))
    c = np.full(128, -1)
    c[0:16] = np.arange(14368, 14384)
    c[32:36] = 8192 + np.array(hs)
    c[64:68] = 8208 + np.array(hs)
    cols.append(c)
    return cols


def a_tile_weights(w_in_l, g):
    cols = np.concatenate(a_weight_cols(g))
    Wsel = np.zeros((4096, cols.size), np.float32)
    m = cols >= 0
    Wsel[:, m] = w_in_l[:, cols[m]]
    return tile_w(Wsel)


def a_consts():
    i = np.arange(C)
    incl = (i[:, None] >= i[None, :])
    cst = {}
    cst["negm"] = np.where(incl, 0.0, -1e30).astype(np.float32)
    cst["negmT"] = np.where(incl.T, 0.0, -1e30).astype(np.float32)
    cst["strict"] = (i[:, None] > i[None, :]).astype(np.float32)
    cst["inclT"] = incl.T.astype(np.float32)
    return np.concatenate([cst["negm"], cst["negmT"], cst["strict"], cst["inclT"]], axis=1)


def emit_A(S, PS, io, NSEQ=4096, stop=99):
    TB = 256
    TPB = TB // 128
    NB = NSEQ // TB
    CPB = TB // C
    with ExitStack() as es:
        S.tes = es
        AR = Arena(S, "ar", 3)
        xin = [AR.view(0, 1), AR.view(0, 1)]
        stage = [AR.view(1 + i, 1) for i in range(2)]
        hTt = S.tile("hT", [128, 32, TB], BF16)
        hT = hTt[:]
        wb = [S.tile(f"wb{i}", [128, 4096], BF16) for i in range(2)]
        ident = S.tile("ident_s", [128, 128], F32)
        ones = S.tile("ones_s", [128, 128], F32)
        vecs = S.tile("vecs_s", [128, 32], F32)
        cw = S.tile("cw_s", [128, 12, 4], F32)
        cst = S.tile("cst_s", [64, 256], F32)
        hv = S.tile("hv_s", [128, 12], F32)
        gn = S.tile("gn_s", [128, 256], F32)
        wa2 = S.tile("wa2_s", [16, 4, 64], F32)
        ba = S.tile("ba_s", [64, 4], F32)
        nba = S.tile("nba_s", [64, 4], F32)
        rmask = S.tile("rmask_s", [64, TB], F32)
        negA = S.tile("negA", [128, 4], F32)
        pre = [S.tile(f"pre{f}", [128, 3 + TB], F32) for f in range(12)]
        fm = [S.tile(f"fm{f}", [128, TB], F32) for f in range(NCHUNKS_A)]
        ssq = [S.tile(f"ssq{i}", [128, TB], F32) for i in range(2)]
        ss = S.tile("ss", [128, 4], F32)
        rs = S.tile("rs", [128, 4], F32)
        bT = [S.tile(f"bT{i}", [64, TB], F32) for i in range(4)]
        qb = [S.tile(f"qb{i}", [64, TB], F32) for i in range(4)]
        kb = [S.tile(f"kb{i}", [64, TB], F32) for i in range(4)]
        kdT = [S.tile(f"kdT{i}", [64, TB], F32) for i in range(4)]
        ebl = [S.tile(f"ebl{i}", [64, CPB], F32) for i in range(4)]
        gtmp = [S.tile(f"gtmp{i}", [64, TB], F32) for i in range(2)]
        Sg = [S.tile(f"Sg{i}", [128, 128], F32) for i in range(4)]
        Sl = [S.tile(f"Sl{i}", [64, 128], F32) for i in range(4)]
        def two(name, shape):
            return [S.tile(f"{name}{k}", shape, F32) for k in range(2)]

        shared2 = [two("mt", [64, 128]), two("beta4", [64, 4]), two("nbeta4", [64, 4]), two("ld4", [64, 4]), two("g4", [64, 4]),
                   two("ng4", [64, 4]), two("eg4", [64, 4]), two("bg4", [64, 4]), two("dk4", [64, 4])]
        diag = S.tile("diag", [64, 4, 64], F32)
        gbc = S.tile("gbc", [128, 4, 64], F32)
        egbc = S.tile("egbc", [128, 4, 64], F32)

        def hb(name, shape):
            return [S.tile(f"{name}{i}", shape, F32) for i in range(4)]

        tA = hb("tA", [64, 64]); dec = hb("dec", [64, 64]); decT = hb("decT", [64, 64])
        Nm = hb("Nm", [64, 64]); Mm = hb("Mm", [64, 64]); Q2 = hb("Q2", [64, 64]); QT2 = hb("QT2", [64, 64])
        X = hb("X", [64, 64]); attnT = hb("attnT", [64, 64])
        vbt = hb("vbt", [64, 128]); kbg = hb("kbg", [64, 128]); u = hb("u", [64, 128]); wT = hb("wT", [128, 64])
        qgT = hb("qgT", [128, 64]); kdec = hb("kdec", [64, 128]); vnew = hb("vnew", [64, 128]); o1 = hb("o1", [64, 128])
        lattn = hb("lattn", [64, 64]); lvtok = hb("lvtok", [64, 128]); lkd = hb("lkd", [64, 64])
        osb = [S.tile(f"osb{i}", [64, 512], F32) for i in range(2)]
        lsb = [S.tile(f"lsb{i}", [64, 512], F32) for i in range(2)]
        oss = S.tile("oss", [64, 8], F32)
        ors = S.tile("ors", [64, 8], F32)
        sz = [S.tile(f"sz{k}", [64, 128], F32) for k in range(8)]
        ojunk = sz
        lo1 = hb("lo1", [64, 128]); tB = hb("tB", [64, 64])

        for (t_, d_) in ((ident, io["ident"]), (vecs, io["vecs"]), (cst, io["cst"]), (hv, io["hv"]), (gn, io["gn"]), (ba, io["ba"]), (rmask, io["rmask"])):
            S.dma("sp", t_[:], d_[:, :])
        S.dma("sp", cw[:], io["cw"][:, :, :])
        S.dma("sp", wa2[:], io["wa2"][:, :, :])
        S.memset(ones[:], 1.0)
        S.ts(nba[:], ba[:], -1.0, ALU.mult)
        S.act(negA[:], hv[:, 0:4], AF.Exp)
        S.ts(negA[:], negA[:], -1.0, ALU.mult)
        for f in range(12):
            S.memset(pre[f][:, 0:3], 0.0)
        for i in range(4):
            S.memset(Sg[i][:], 0.0)
            S.memset(Sl[i][:], 0.0)
        negm = cst[:, 0:64]; negmT = cst[:, 64:128]; strict = cst[:, 128:192]; inclT = cst[:, 192:256]
        id64 = ident[0:64, 0:64]

        wab = S.tile("wab_i", [NCHUNKS_A, 128, 4096], BF16, "dram")
        stdep = [Dep("wst0"), Dep("wst1")]
        for f in range(NCHUNKS_A):
            i = f % 2
            S.dma("sp", stage[i], io["wa"][f])
            S.copy(wb[i][:], stage[i], eng=("act" if f % 2 else "dve"))
            S.dma("pool", wab.sub(f, f), wb[i][:], owner_deps=[stdep[i]])
        S.barrier()
        wbr = [wb[0][:], wb[1][:], AR.view(1, 1, BF16)[:, 0:4096], AR.view(2, 1, BF16)[:, 0:4096]]
        wcnt = [0]

        def getw(f, K):
            i = wcnt[0] % 4
            wcnt[0] += 1
            S.dma("sp", wbr[i][:, 0:K], wab[f])
            return wbr[i]

        trc = [0]

        def to_feat(src, dst, tok_off, nk, wcol):
            for g in range(0, nk, 4):
                n = min(4, nk - g)
                bank = PS[trc[0] % 2]
                trc[0] += 1
                for j in range(n):
                    S.tr(bank[:, j * 128:(j + 1) * 128], src[:, (g + j) * 128:(g + j + 1) * 128], ident[:])
                o = dst[:, g:g + n, tok_off:tok_off + 128]
                i_ = bank[:, 0:n * 128].rearrange("p (k t) -> p k t", k=n)
                w_ = wcol[:, g:g + n].rearrange("p (k o) -> p k o", o=1).to_broadcast([128, n, 128])
                S.tt(o, i_, w_, ALU.mult)

        evc = [0]

        def evac(dst, src):
            evc[0] += 1
            S.copy(dst, src, eng=("act" if evc[0] % 2 else "dve"))

        for blk in range(NB):
            r0 = blk * TB
            for t in range(TPB):
                xv = xin[t % 2]
                io["load_x"](xv, blk * TPB + t)
                S.act(wb[0][:], xv, AF.Square, accum_out=ss[:, t:t + 1])
                S.ts(ss[:, t:t + 1], ss[:, t:t + 1], 1.0 / 4096, ALU.mult, EPS, ALU.add)
                S.act(ss[:, t:t + 1], ss[:, t:t + 1], AF.Sqrt)
                S.recip(rs[:, t:t + 1], ss[:, t:t + 1])
                S.ts(xv, xv, rs[:, t:t + 1], ALU.mult)
                to_feat(xv, hT, t * 128, 32, vecs[:, 0:32])
            for f in range(NCHUNKS_A):
                w = getw(f, 4096)
                bank = PS[2 + f % 2]
                for kc in range(32):
                    S.mm(bank[:, 0:TB], w[:, kc * 128:(kc + 1) * 128], hT[:, kc, :], start=(kc == 0), stop=(kc == 31))
                if f < 12:
                    evac(pre[f][:, 3:3 + TB], bank[:, 0:TB])
                else:
                    evac(fm[f][:], bank[:, 0:TB])
            if stop < 1:
                continue
            for f in range(12):
                o = fm[f]
                S.ts(o[:], pre[f][:, 0:TB], cw[:, f, 0:1], ALU.mult)
                for k in range(1, 4):
                    S.stt(o[:], pre[f][:, k:k + TB], cw[:, f, k:k + 1], o[:], ALU.mult, ALU.add)
                S.copy(pre[f][:, 0:3], pre[f][:, TB:TB + 3], eng="pool")
                S.act(o[:], o[:], AF.Silu)
                if f < 8:
                    sq = ssq[f % 2]
                    S.act(sq[:], o[:], AF.Square)
                    bank = PS[4 + f % 2]
                    S.mm(bank[:, 0:TB], ones[:], sq[:])
                    S.ts(sq[:], bank[:, 0:TB], EPS, ALU.add)
                    S.act(sq[:], sq[:], AF.Sqrt)
                    S.recip(sq[:], sq[:])
                    if f < 4:
                        S.stt(o[:], o[:], 128 ** -0.5, sq[:], ALU.mult, ALU.mult)
                    else:
                        S.tt(o[:], o[:], sq[:], ALU.mult)
            if stop < 2:
                continue
            for i in range(4):
                bank = PS[4 + i % 2]
                S.mm(bank[0:64, 0:TB], wa2[:, i, :], fm[CM][0:16, :])
                t_ = gtmp[i % 2]
                S.act(t_[:], bank[0:64, 0:TB], AF.Exp, bias=nba[:, i:i + 1], scale=-1.0)
                S.act(t_[:], t_[:], AF.Ln, bias=1.0)
                S.ts(t_[:], t_[:], -1.0 / 16.0, ALU.mult)
                S.op("dve", lambda e, i=i, t_=t_: e.tensor_tensor_scan(_ap(bT[i][:]), _ap(rmask[:]), _ap(t_[:]), 0.0, ALU.mult, ALU.add),
                     reads=[rmask[:], t_[:]], writes=[bT[i][:]])
                eb = gtmp[(i + 1) % 2]
                S.act(eb[:], bT[i][:], AF.Exp)
                S.stt(qb[i][:], fm[LQ + i][0:64, :], 64 ** -0.5, eb[:], ALU.mult, ALU.mult)
                S.act(eb[:], bT[i][:], AF.Exp, scale=-1.0)
                S.tt(kb[i][:], fm[LK + i][0:64, :], eb[:], ALU.mult)
                b3 = bT[i][:].rearrange("p (c t) -> p c t", t=C)
                S.tt(eb[:].rearrange("p (c t) -> p c t", t=C), b3[:, :, C - 1:C].to_broadcast([64, CPB, C]), b3, ALU.subtract)
                S.act(eb[:], eb[:], AF.Exp)
                S.tt(kdT[i][:], fm[LK + i][0:64, :], eb[:], ALU.mult)
                S.act(ebl[i][:], b3[:, :, C - 1], AF.Exp)
            if stop < 3:
                continue
            for c in range(CPB):
                cs = slice(c * C, (c + 1) * C)
                par = c % 2
                tok0 = r0 + c * C
                mt, beta4, nbeta4, ld4, g4, ng4, eg4, bg4, dk4 = [x[par] for x in shared2]
                bank = PS[4]
                S.tr(bank[0:64, 0:128], fm[CM][:, cs], ident[:])
                S.copy(mt[:], bank[0:64, 0:128], eng="act")
                S.act(beta4[:], mt[:, 32:36], AF.Sigmoid)
                S.ts(nbeta4[:], beta4[:], -1.0, ALU.mult)
                S.tt(ld4[:], mt[:, 64:68], hv[0:64, 4:8], ALU.add)
                S.act(ld4[:], ld4[:], AF.Exp)
                S.act(ld4[:], ld4[:], AF.Ln, bias=1.0)
                S.tt(ld4[:], ld4[:], negA[0:64, :], ALU.mult)
                S.mm(bank[0:64, 128:132], inclT, ld4[:])
                S.copy(g4[:], bank[0:64, 128:132], eng="act")
                S.ts(ng4[:], g4[:], -1.0, ALU.mult)
                S.act(eg4[:], g4[:], AF.Exp)
                S.tt(bg4[:], eg4[:], beta4[:], ALU.mult)
                S.tt(diag[:], id64.rearrange("p (h t) -> p h t", h=1).to_broadcast([64, 4, 64]),
                     g4[:].rearrange("p (h o) -> p h o", o=1).to_broadcast([64, 4, 64]), ALU.mult)
                bank = PS[5]
                S.mm(bank[:, 0:256], ones[0:64, :], diag[:].rearrange("p h t -> p (h t)"))
                S.copy(gbc[:].rearrange("p h t -> p (h t)"), bank[:, 0:256], eng="act")
                S.act(egbc[:].rearrange("p h t -> p (h t)"), bank[:, 0:256], AF.Exp)
                S.tt(dk4[:], gbc[0:64, :, C - 1], g4[:], ALU.subtract)
                S.act(dk4[:], dk4[:], AF.Exp)

                def gdn_gen(i):
                    kTc = fm[CK + i][:, cs]; qTc = fm[CQ + i][:, cs]; vTc = fm[CV + i][:, cs]
                    pb = PS[i]
                    S.tr(pb[0:64, 0:128], kTc, ident[:]); yield
                    S.tr(pb[0:64, 128:256], vTc, ident[:]); yield
                    S.ts(kdec[i][:], pb[0:64, 0:128], dk4[:, i:i + 1], ALU.mult); yield
                    S.ts(kbg[i][:], pb[0:64, 0:128], bg4[:, i:i + 1], ALU.mult); yield
                    S.ts(vbt[i][:], pb[0:64, 128:256], beta4[:, i:i + 1], ALU.mult); yield
                    S.stt(tA[i][:], gbc[0:64, i, :], -1.0, negm, ALU.mult, ALU.add); yield
                    S.act(dec[i][:], tA[i][:], AF.Exp, bias=g4[:, i:i + 1]); yield
                    S.tt(tB[i][:], gbc[0:64, i, :], negmT, ALU.add); yield
                    S.act(decT[i][:], tB[i][:], AF.Exp, bias=ng4[:, i:i + 1]); yield
                    S.mm(pb[0:64, 256:320], kTc, kTc); yield
                    S.mm(pb[0:64, 320:384], kTc, qTc); yield
                    S.stt(Nm[i][:], pb[0:64, 256:320], nbeta4[:, i:i + 1], dec[i][:], ALU.mult, ALU.mult); yield
                    S.tt(Nm[i][:], Nm[i][:], strict, ALU.mult); yield
                    S.tt(attnT[i][:], pb[0:64, 320:384], decT[i][:], ALU.mult); yield
                    S.tr(pb[0:64, 384:448], Nm[i][:], id64); yield
                    S.copy(Mm[i][:], pb[0:64, 384:448], eng="act"); yield
                    S.tt(X[i][:], Mm[i][:], id64, ALU.add); yield
                    Q, QT = Mm[i], Nm[i]
                    Qn, QTn = Q2[i], QT2[i]
                    for st in range(5):
                        S.mm(pb[0:64, 0:64], Q[:], QT[:]); yield
                        if st < 4:
                            S.mm(pb[0:64, 64:128], QT[:], Q[:]); yield
                        S.copy(QTn[:], pb[0:64, 0:64], eng="act"); yield
                        if st < 4:
                            S.copy(Qn[:], pb[0:64, 64:128], eng="dve"); yield
                        S.mm(pb[0:64, 128:192], QTn[:], X[i][:]); yield
                        S.tt(X[i][:], X[i][:], pb[0:64, 128:192], ALU.add); yield
                        Q, QT, Qn, QTn = Qn, QTn, Q, QT
                    S.mm(pb[0:64, 192:320], X[i][:], vbt[i][:]); yield
                    S.mm(pb[:, 320:384], kbg[i][:], X[i][:]); yield
                    S.copy(u[i][:], pb[0:64, 192:320], eng="act"); yield
                    S.copy(wT[i][:], pb[:, 320:384], eng="dve"); yield
                    S.tt(qgT[i][:], qTc, egbc[:, i, :], ALU.mult); yield
                    S.mm(pb[0:64, 0:128], wT[i][:], Sg[i][:]); yield
                    S.tt(vnew[i][:], u[i][:], pb[0:64, 0:128], ALU.subtract); yield
                    S.mm(pb[0:64, 128:256], qgT[i][:], Sg[i][:]); yield
                    S.copy(o1[i][:], pb[0:64, 128:256], eng="act"); yield
                    S.mm(pb[0:64, 256:384], attnT[i][:], vnew[i][:]); yield
                    S.mm(pb[:, 384:512], kdec[i][:], vnew[i][:]); yield
                    S.tt(o1[i][:], o1[i][:], pb[0:64, 256:384], ALU.add); yield
                    S.stt(Sg[i][:], Sg[i][:], egbc[:, i, C - 1:C], pb[:, 384:512], ALU.mult, ALU.add); yield
                    S.act(ojunk[i][:], o1[i][:], AF.Square, accum_out=oss[:, i:i + 1]); yield
                    S.ts(oss[:, i:i + 1], oss[:, i:i + 1], 1.0 / 128, ALU.mult, EPS, ALU.add); yield
                    S.act(oss[:, i:i + 1], oss[:, i:i + 1], AF.Sqrt); yield
                    S.recip(ors[:, i:i + 1], oss[:, i:i + 1]); yield
                    S.stt(o1[i][:], o1[i][:], ors[:, i:i + 1], gn[0:64, 0:128], ALU.mult, ALU.mult); yield
                    S.tr(pb[0:64, 0:128], fm[CZ + i][:, cs], ident[:]); yield
                    S.act(sz[i][:], pb[0:64, 0:128], AF.Silu); yield
                    S.tt(osb[par][:, i * 128:(i + 1) * 128], o1[i][:], sz[i][:], ALU.mult); yield

                def gla_gen(i):
                    pb = PS[4 + i]
                    S.mm(pb[0:64, 0:64], kb[i][:, cs], qb[i][:, cs]); yield
                    S.tt(lattn[i][:], pb[0:64, 0:64], inclT, ALU.mult); yield
                    S.tr(pb[0:64, 64:192], fm[LV + i][:, cs], ident[:]); yield
                    S.copy(lvtok[i][:], pb[0:64, 64:192], eng="act"); yield
                    S.tr(pb[0:64, 192:256], kdT[i][:, cs], id64); yield
                    S.copy(lkd[i][:], pb[0:64, 192:256], eng="dve"); yield
                    S.mm(pb[0:64, 384:512], qb[i][:, cs], Sl[i][:]); yield
                    S.copy(lo1[i][:], pb[0:64, 384:512], eng="act"); yield
                    S.mm(pb[0:64, 0:128], lattn[i][:], lvtok[i][:]); yield
                    S.tt(lo1[i][:], lo1[i][:], pb[0:64, 0:128], ALU.add); yield
                    S.mm(pb[0:64, 128:256], lkd[i][:], lvtok[i][:]); yield
                    S.stt(Sl[i][:], Sl[i][:], ebl[i][:, c:c + 1], pb[0:64, 128:256], ALU.mult, ALU.add); yield
                    S.act(ojunk[4 + i][:], lo1[i][:], AF.Square, accum_out=oss[:, 4 + i:5 + i]); yield
                    S.ts(oss[:, 4 + i:5 + i], oss[:, 4 + i:5 + i], 1.0 / 128, ALU.mult, EPS, ALU.add); yield
                    S.act(oss[:, 4 + i:5 + i], oss[:, 4 + i:5 + i], AF.Sqrt); yield
                    S.recip(ors[:, 4 + i:5 + i], oss[:, 4 + i:5 + i]); yield
                    S.stt(lo1[i][:], lo1[i][:], ors[:, 4 + i:5 + i], gn[0:64, 128:256], ALU.mult, ALU.mult); yield
                    S.tr(pb[0:64, 256:384], fm[LR + i][:, cs], ident[:]); yield
                    S.act(sz[4 + i][:], pb[0:64, 256:384], AF.Silu); yield
                    S.tt(lsb[par][:, i * 128:(i + 1) * 128], lo1[i][:], sz[4 + i][:], ALU.mult); yield

                gens = [gdn_gen(i) for i in range(4)] + [gla_gen(i) for i in range(4)]
                while gens:
                    for gnr in list(gens):
                        try:
                            next(gnr)
                        except StopIteration:
                            gens.remove(gnr)
                io["store_og"](tok0, osb[par][:])
                io["store_ol"](tok0, lsb[par][:])
        S.barrier()


def build_A(NSEQ=4096, stop=99):
    TB = 256
    nc = bass.Bass("TRN2", target_bir_lowering=False)

    def din(name, shape, dt=F32):
        return nc.dram_tensor(name, list(shape), dt, kind="ExternalInput").ap()

    x_d = din("x", [NSEQ, 4096])
    io = dict(wa=din("wa", [NCHUNKS_A, 128, 4096]), vecs=din("vecs", [128, 32]), cw=din("cw", [128, 12, 4]),
              cst=din("cst", [64, 256]), ident=din("ident", [128, 128]), hv=din("hv", [128, 12]), gn=din("gn", [128, 256]),
              wa2=din("wa2", [16, 4, 64]), ba=din("ba", [64, 4]), rmask=din("rmask", [64, TB]))
    og_d = nc.dram_tensor("og", [NSEQ, 512], F32, kind="ExternalOutput").ap()
    ol_d = nc.dram_tensor("ol", [NSEQ, 512], F32, kind="ExternalOutput").ap()
    with ExitStack() as es0:
        S = Sched(nc, es0)
        PS = [S.tile(f"ps{i}", [128, 512], F32, "psum") for i in range(8)]
        io["load_x"] = lambda xv, t: S.dma("sp", xv, x_d[t * 128:(t + 1) * 128, :])
        io["store_og"] = lambda tok0, v: S.dma("pool", og_d[tok0:tok0 + C, :], v)
        io["store_ol"] = lambda tok0, v: S.dma("pool", ol_d[tok0:tok0 + C, :], v)
        emit_A(S, PS, io, NSEQ=NSEQ, stop=stop)
        S.finish("pool")
        print("A instrs", S.ninstr, "sems", S.nsem)
    return nc


def _a_inputs(inp, L, g):
    hs = [4 * g + i for i in range(4)]
    cols = a_weight_cols(g)
    conv = inp["gdn_conv"][L]
    cw = np.stack([conv[:, cols[f]].T for f in range(12)], axis=1).astype(np.float32)
    hv = np.zeros((128, 12), np.float32)
    hv[:, 0:4] = inp["gdn_a_log"][L][hs][None]
    hv[:, 4:8] = inp["gdn_dt_bias"][L][hs][None]
    gn = np.concatenate([np.tile(inp["gdn_norm"][L][None], (128, 1)),
                         np.tile(inp["gla_norm"][L][None], (128, 1))], axis=1).astype(np.float32)
    wa2 = np.stack([inp["gla_w_a2"][L][:, h * 64:(h + 1) * 64] for h in hs], axis=1).astype(np.float32)
    ba = np.stack([inp["gla_b_a"][L][h * 64:(h + 1) * 64] for h in hs], axis=1).astype(np.float32)
    rm = np.ones((64, 256), np.float32)
    rm[:, ::64] = 0.0
    return dict(wa=a_tile_weights(inp["w_in"][L], g),
                vecs=np.ascontiguousarray(inp["norm_mix"][L].reshape(32, 128).T), cw=np.ascontiguousarray(cw),
                cst=a_consts(), ident=np.eye(128, dtype=np.float32), hv=hv, gn=gn,
                wa2=np.ascontiguousarray(wa2), ba=np.ascontiguousarray(ba), rmask=rm)


def _b_weights(inp, L):
    def vec(v):
        return np.ascontiguousarray(v.reshape(32, 128).T)
    w_in = inp["w_in"][L]
    return dict(
        wg=tile_w(w_in[:, 14384:]), wog=tile_w(inp["w_o_gdn"][L]), wol=tile_w(inp["w_o_gla"][L]),
        wout=tile_w(inp["w_out"][L]), wq=tile_w(inp["peer_w_query"][L]), wpg=tile_w(inp["w_ple_gate"][L]),
        wple=tile_w(inp["w_ple"][L]),
        kT=np.ascontiguousarray(np.stack([inp["peer_keys1"][L].T, inp["peer_keys2"][L].T]).astype(np.float32)),
        pu=np.ascontiguousarray(inp["peer_u"][L]), pv=np.ascontiguousarray(inp["peer_v"][L]),
        vecs=np.ascontiguousarray(np.concatenate([vec(inp["norm_mix"][L]), vec(inp["norm_ffn"][L]), vec(inp["norm_ple"][L])], axis=1)),
        nffn=np.ascontiguousarray(inp["norm_ffn"][L]), nfin=np.ascontiguousarray(inp["norm_final"]),
        ident=np.eye(128, dtype=np.float32), iota16=np.tile(np.arange(16, dtype=np.float32), (128, 1)),
    )


def kernel(**inp):
    inp = {k: np.asarray(v) for k, v in inp.items()}
    NCORE = 8
    x = np.ascontiguousarray(inp["x"], dtype=np.float32)
    Bn, Sn, Dn = x.shape
    ncA = build_A(NSEQ=Sn)
    ncB = {False: build_B(final=False), True: None}
    for L in range(2):
        ga = [_a_inputs(inp, L, g) for g in range(4)]
        in_maps = []
        for c in range(NCORE):
            b, g = divmod(c, 4)
            m = dict(ga[g])
            m["x"] = np.ascontiguousarray(x[b])
            in_maps.append(m)
        resA = run_bass_kernel_spmd(ncA, in_maps, core_ids=list(range(NCORE))).results
        del ga, in_maps
        og = np.empty((Bn, Sn, 2048), np.float32)
        ol = np.empty((Bn, Sn, 2048), np.float32)
        for c in range(NCORE):
            b, g = divmod(c, 4)
            og[b, :, g * 512:(g + 1) * 512] = resA[c]["og"]
            ol[b, :, g * 512:(g + 1) * 512] = resA[c]["ol"]
        final = (L == 1)
        if ncB.get(final) is None:
            ncB[final] = build_B(final=final)
        wB = _b_weights(inp, L)
        xf = x.reshape(Bn * Sn, Dn)
        ogf = og.reshape(Bn * Sn, 2048)
        olf = ol.reshape(Bn * Sn, 2048)
        pf = np.ascontiguousarray(inp["p"][L]).reshape(Bn * Sn, 256)
        in_maps = []
        for c in range(NCORE):
            sl = slice(c * 1024, (c + 1) * 1024)
            m = dict(wB)
            m.update(x=np.ascontiguousarray(xf[sl]), og=np.ascontiguousarray(ogf[sl]), ol=np.ascontiguousarray(olf[sl]),
                     p=np.ascontiguousarray(pf[sl]))
            in_maps.append(m)
        resB = run_bass_kernel_spmd(ncB[final], in_maps, core_ids=list(range(NCORE))).results
        del in_maps, wB
        x = np.concatenate([resB[c]["xo"] for c in range(NCORE)], axis=0).reshape(Bn, Sn, Dn)
    return np.ascontiguousarray(x, dtype=np.float32)


def build_fused():
    NSEQ, NTOK = 4096, 1024
    nc = bass.Bass("TRN2", target_bir_lowering=False)

    def din(name, shape, dt=F32):
        return nc.dram_tensor(name, list(shape), dt, kind="ExternalInput").ap()

    xseq_d = din("xseq", [NSEQ, 4096])
    xtok_d = din("xtok", [NTOK, 4096])
    p_d = din("p", [2, NTOK, 256])
    A_in = dict(wa=din("wa", [2, NCHUNKS_A, 128, 4096]), vecs=din("vecsA", [2, 128, 32]), cw=din("cw", [2, 128, 12, 4]),
                hv=din("hv", [2, 128, 12]), gn=din("gn", [2, 128, 256]), wa2=din("wa2", [2, 16, 4, 64]), ba=din("ba", [2, 64, 4]))
    cst_d = din("cst", [64, 256]); ident_d = din("ident", [128, 128]); rmask_d = din("rmask", [64, 256])
    B_in = dict(wg=din("wg", [2, 64, 128, 4096]), wog=din("wog", [2, 32, 128, 2048]), wol=din("wol", [2, 32, 128, 2048]),
                wout=din("wout", [2, 32, 128, 4096]), wq=din("wq", [2, 16, 128, 4096]), wpg=din("wpg", [2, 32, 128, 4096]),
                wple=din("wple", [2, 32, 128, 256]), kT=din("kT", [2, 2, 128, 128]),
                vecs=din("vecsB", [2, 128, 96]), nffn=din("nffn", [2, 4096]))
    nfin_d = din("nfin", [4096]); iota_d = din("iota16", [128, 16])
    pu_d = [din(f"pu{L}", [16384, 4096]) for L in range(2)]
    pv_d = [din(f"pv{L}", [16384, 4096]) for L in range(2)]
    xidx_d = din("xidx", [128, 32], I32)
    oidx_d = din("oidx", [128, 8, 4], I32)
    out_d = nc.dram_tensor("xo", [NTOK, 4096], F32, kind="ExternalOutput").ap()

    with ExitStack() as es0:
        S = Sched(nc, es0)
        PS = [S.tile(f"ps{i}", [128, 512], F32, "psum") for i in range(8)]
        oa = S.tile("oa_i", [NSEQ, 1024], F32, "dram")
        OA = S.tile("OA_i", [8 * NSEQ, 1024], F32, "dram")
        xb = S.tile("xb_i", [NTOK, 4096], F32, "dram")
        XG = S.tile("XG_i", [8 * NTOK, 4096], F32, "dram")
        xidx = S.tile("xidx_s", [128, 32], I32)
        oidx = S.tile("oidx_s", [128, 8, 4], I32)
        S.dma("sp", xidx[:], xidx_d[:, :])
        S.dma("sp", oidx[:], oidx_d[:, :, :])

        for L in range(2):
            ioA = {k: v[L] for k, v in A_in.items()}
            ioA.update(cst=cst_d, ident=ident_d, rmask=rmask_d)
            if L == 0:
                ioA["load_x"] = lambda xv, t: S.dma("sp", xv, xseq_d[t * 128:(t + 1) * 128, :])
            else:
                ioA["load_x"] = lambda xv, t: S.dma("pool", xv, XG[:, :], indirect=xidx[:, t:t + 1])

            def store_o(tok0, v, col0):
                dst = oa.sub((col0, tok0), (slice(tok0, tok0 + C), slice(col0, col0 + 512)))
                S.dma("pool", dst, v, owner_deps=_deps([v]))

            ioA["store_og"] = lambda tok0, v: store_o(tok0, v, 0)
            ioA["store_ol"] = lambda tok0, v: store_o(tok0, v, 512)
            emit_A(S, PS, ioA, NSEQ=NSEQ)
            S.collective("AllGather", View(oa[:, :].ap, tuple(oa.subs.values())), OA[:, :])

            ioB = {k: v[L] for k, v in B_in.items()}
            ioB.update(ident=ident_d, iota16=iota_d, nfin=nfin_d, pu=pu_d[L], pv=pv_d[L])

            def load_o(tmpx, t):
                for g in range(4):
                    S.dma("pool", tmpx[:, g * 1024:(g + 1) * 1024], OA[:, :], indirect=oidx[:, t, g:g + 1])
                return ((lambda k: tmpx[:, (k // 4) * 1024 + (k % 4) * 128:(k // 4) * 1024 + (k % 4 + 1) * 128]),
                        (lambda k: tmpx[:, (k // 4) * 1024 + 512 + (k % 4) * 128:(k // 4) * 1024 + 512 + (k % 4 + 1) * 128]))

            ioB["load_o"] = load_o
            if L == 0:
                ioB["load_x"] = lambda v, t: S.dma("sp", v, xtok_d[t * 128:(t + 1) * 128, :])
                ioB["store_x"] = lambda t, v: S.dma("pool", xb.sub(t, (slice(t * 128, (t + 1) * 128), slice(None))), v,
                                                    owner_deps=_deps([v]))
            else:
                ioB["load_x"] = lambda v, t: S.dma("sp", v, xb.sub(t, (slice(t * 128, (t + 1) * 128), slice(None))))
                ioB["store_x"] = lambda t, v: S.dma("pool", out_d[t * 128:(t + 1) * 128, :], v)
            ioB["load_p"] = lambda v, t, L=L: S.dma("sp", v, p_d[L, t * 128:(t + 1) * 128, :])
            emit_B(S, PS, ioB, final=(L == 1), NTOK=NTOK)
            if L == 0:
                S.collective("AllGather", View(xb[:, :].ap, tuple(xb.subs.values())), XG[:, :])
        S.finish("pool")
        print("fused instrs", S.ninstr, "sems", S.nsem)
    return nc


def kernel_fused_experimental(**inp):
    inp = {k: np.asarray(v) for k, v in inp.items()}
    NCORE = 8
    x = np.ascontiguousarray(inp["x"], dtype=np.float32)
    Bn, Sn, Dn = x.shape
    nc = build_fused()
    ga = [[_a_inputs(inp, L, g) for g in range(4)] for L in range(2)]
    wB = [_b_weights(inp, L) for L in range(2)]
    shared = {}
    for k in ("wg", "wog", "wol", "wout", "wq", "wpg", "wple", "kT", "nffn"):
        shared[k] = np.stack([wB[0][k], wB[1][k]])
    for L in range(2):
        shared[f"pu{L}"] = wB[L]["pu"]
        shared[f"pv{L}"] = wB[L]["pv"]
    shared["vecsB"] = np.stack([wB[0]["vecs"], wB[1]["vecs"]])
    shared["nfin"] = wB[0]["nfin"]
    shared["iota16"] = wB[0]["iota16"]
    shared["ident"] = wB[0]["ident"]
    shared["cst"] = ga[0][0]["cst"]
    shared["rmask"] = ga[0][0]["rmask"]
    del wB
    xf = x.reshape(Bn * Sn, Dn)
    in_maps = []
    pp = np.arange(128)
    for c in range(NCORE):
        b, g = divmod(c, 4)
        m = dict(shared)
        for k, kk in (("wa", "wa"), ("vecsA", "vecs"), ("cw", "cw"), ("hv", "hv"), ("gn", "gn"), ("wa2", "wa2"), ("ba", "ba")):
            m[k] = np.stack([ga[0][g][kk], ga[1][g][kk]])
        m["xseq"] = np.ascontiguousarray(x[b])
        m["xtok"] = np.ascontiguousarray(xf[c * 1024:(c + 1) * 1024])
        m["p"] = np.ascontiguousarray(np.stack([inp["p"][L].reshape(Bn * Sn, 256)[c * 1024:(c + 1) * 1024] for L in range(2)]))
        m["xidx"] = np.ascontiguousarray((b * Sn + np.arange(32)[None, :] * 128 + pp[:, None]).astype(np.int32))
        oi = ((b * 4 + np.arange(4))[None, None, :] * Sn + g * 1024 + np.arange(8)[None, :, None] * 128 + pp[:, None, None])
        m["oidx"] = np.ascontiguousarray(oi.astype(np.int32))
        in_maps.append(m)
    res = run_bass_kernel_spmd(nc, in_maps, core_ids=list(range(NCORE))).results
    out = np.concatenate([res[c]["xo"] for c in range(NCORE)], axis=0).reshape(Bn, Sn, Dn)
    return np.ascontiguousarray(out, dtype=np.float32)
```
